# Optimizing a Trainium2 kernel written in Bass

```python
import jax, jax.numpy as jnp
from jax import lax
import numpy as np

D_MODEL = 2048
BATCH = 4
SEQ = 2048
DEPTH = 4

N_MIXERS = 4
HEAD_DIM = 128
N_HEADS = D_MODEL // HEAD_DIM
N_KV_HEADS = 4
GROUP = N_HEADS // N_KV_HEADS
INNER = N_HEADS * HEAD_DIM
KV_WIDTH = N_KV_HEADS * HEAD_DIM
IDX_HEADS = 16
IDX_DIM = 64
DSA_TOPK = 256
MOBA_BLOCK = 256
MOBA_TOPK = 3
RET_HEADS = 8
RET_QK_DIM = D_MODEL // RET_HEADS
RET_V_DIM = 2 * RET_QK_DIM
RET_INNER = RET_HEADS * RET_V_DIM
RET_CHUNK = 128
Q_BLOCK = 128
ROPE_THETA = 10000.0
EPS = 1e-6
NEG = -1e30

A_WIDTHS = (INNER, KV_WIDTH, KV_WIDTH, INNER, IDX_HEADS * IDX_DIM, IDX_DIM, IDX_HEADS)
B_WIDTHS = (INNER, KV_WIDTH, KV_WIDTH, INNER)
C_WIDTHS = (RET_HEADS * RET_QK_DIM, RET_HEADS * RET_QK_DIM, RET_INNER, RET_INNER)
D_WIDTHS = (INNER, INNER, INNER, INNER, N_HEADS)

kernel_name = "hybrid_dsa_moba_retnet_fox_trunk"


def n_layers_of(m):
    return len(range(m, DEPTH, N_MIXERS))


def split_cols(z, widths):
    cuts = [int(c) for c in np.cumsum(widths)[:-1]]
    return jnp.split(z, cuts, axis=-1)


def rms_norm(x, g):
    xf = x.astype(jnp.float32)
    y = xf * lax.rsqrt(jnp.mean(xf * xf, axis=-1, keepdims=True) + EPS)
    return (y * g.astype(jnp.float32)).astype(x.dtype)


def rope(x, pos):
    d = x.shape[-1]
    inv = ROPE_THETA ** (-jnp.arange(0, d, 2, dtype=jnp.float32) / d)
    ang = pos.astype(jnp.float32)[:, None] * inv[None, :]
    cos = jnp.cos(ang)[None, :, None, :]
    sin = jnp.sin(ang)[None, :, None, :]
    x1, x2 = jnp.split(x.astype(jnp.float32), 2, axis=-1)
    return jnp.concatenate([x1 * cos - x2 * sin, x1 * sin + x2 * cos], axis=-1).astype(x.dtype)


def sweep_query_blocks(fn, T):
    starts = jnp.arange(T // Q_BLOCK, dtype=jnp.int32) * Q_BLOCK
    out = lax.map(fn, starts)
    _, B, _, w = out.shape
    return jnp.swapaxes(out, 0, 1).reshape(B, T, w)


def batched_gather(src, idx):
    return jax.vmap(lambda s, i: s[i])(src, idx)


def dsa_mixer(h, w_in, q_g, k_g, w_out):
    B, T, _ = h.shape
    pos = jnp.arange(T, dtype=jnp.int32)
    q, k, v, gate, qi, ki, wi = split_cols(h @ w_in, A_WIDTHS)
    q = rope(rms_norm(q.reshape(B, T, N_HEADS, HEAD_DIM), q_g), pos)
    k = rope(rms_norm(k.reshape(B, T, N_KV_HEADS, HEAD_DIM), k_g), pos)
    v = v.reshape(B, T, N_KV_HEADS, HEAD_DIM)
    qi = rope(qi.reshape(B, T, IDX_HEADS, IDX_DIM), pos).astype(jnp.float32)
    ki = rope(ki.reshape(B, T, 1, IDX_DIM), pos)[:, :, 0].astype(jnp.float32)
    wi = wi.astype(jnp.float32) * IDX_HEADS ** -0.5
    n_keep = min(DSA_TOPK, T // 4)
    scale = HEAD_DIM ** -0.5

    def block(start):
        tq = start + jnp.arange(Q_BLOCK, dtype=jnp.int32)
        qib = lax.dynamic_slice_in_dim(qi, start, Q_BLOCK, axis=1)
        wib = lax.dynamic_slice_in_dim(wi, start, Q_BLOCK, axis=1)
        rel = jax.nn.relu(jnp.einsum('bqhd,bsd->bqhs', qib, ki) * IDX_DIM ** -0.5)
        score = jnp.einsum('bqh,bqhs->bqs', wib, rel)
        causal = pos[None, :] <= tq[:, None]
        score = jnp.where(causal[None], score, NEG)
        _, sel = lax.top_k(score, n_keep)
        valid = sel <= tq[None, :, None]
        k_sel = batched_gather(k, sel)
        v_sel = batched_gather(v, sel)
        qb = lax.dynamic_slice_in_dim(q, start, Q_BLOCK, axis=1).reshape(B, Q_BLOCK, N_KV_HEADS, GROUP, HEAD_DIM)
        logits = jnp.einsum('bqgrd,bqkgd->bqgrk', qb, k_sel).astype(jnp.float32) * scale
        logits = jnp.where(valid[:, :, None, None, :], logits, NEG)
        p = jax.nn.softmax(logits, axis=-1).astype(v.dtype)
        o = jnp.einsum('bqgrk,bqkgd->bqgrd', p, v_sel)
        return o.reshape(B, Q_BLOCK, INNER)

    o = sweep_query_blocks(block, T)
    return (jax.nn.silu(gate) * o) @ w_out


def moba_mixer(h, w_in, q_g, k_g, w_out):
    B, T, _ = h.shape
    pos = jnp.arange(T, dtype=jnp.int32)
    q, k, v, gate = split_cols(h @ w_in, B_WIDTHS)
    q = rope(rms_norm(q.reshape(B, T, N_HEADS, HEAD_DIM), q_g), pos)
    k = rope(rms_norm(k.reshape(B, T, N_KV_HEADS, HEAD_DIM), k_g), pos)
    v = v.reshape(B, T, N_KV_HEADS, HEAD_DIM)
    n_blocks = -(-T // MOBA_BLOCK)
    pad = n_blocks * MOBA_BLOCK - T
    kb = jnp.pad(k, ((0, 0), (0, pad), (0, 0), (0, 0))).reshape(B, n_blocks, MOBA_BLOCK, N_KV_HEADS, HEAD_DIM)
    vb = jnp.pad(v, ((0, 0), (0, pad), (0, 0), (0, 0))).reshape(B, n_blocks, MOBA_BLOCK, N_KV_HEADS, HEAD_DIM)
    k_mean = jnp.mean(kb.astype(jnp.float32), axis=2)
    kbt = kb.transpose(0, 3, 1, 2, 4)
    vbt = vb.transpose(0, 3, 1, 2, 4)
    n_sel = min(MOBA_TOPK, n_blocks - 1)
    gather2 = jax.vmap(jax.vmap(lambda s, i: s[i]))
    scale = HEAD_DIM ** -0.5

    def block(start):
        tq = start + jnp.arange(Q_BLOCK, dtype=jnp.int32)
        own = start // MOBA_BLOCK
        qb = lax.dynamic_slice_in_dim(q, start, Q_BLOCK, axis=1).reshape(B, Q_BLOCK, N_KV_HEADS, GROUP, HEAD_DIM)
        k_own = lax.dynamic_index_in_dim(kb, own, axis=1, keepdims=False)
        v_own = lax.dynamic_index_in_dim(vb, own, axis=1, keepdims=False)
        own_pos = own * MOBA_BLOCK + jnp.arange(MOBA_BLOCK, dtype=jnp.int32)
        l_own = jnp.einsum('bqgrd,bsgd->bqgrs', qb, k_own).astype(jnp.float32) * scale
        l_own = jnp.where((own_pos[None, :] <= tq[:, None])[None, :, None, None, :], l_own, NEG)
        if n_sel == 0:
            p = jax.nn.softmax(l_own, axis=-1).astype(v.dtype)
            o = jnp.einsum('bqgrs,bsgd->bqgrd', p, v_own)
        else:
            gscore = jnp.einsum('bqgrd,bngd->bqgn', qb.astype(jnp.float32), k_mean)
            past = jnp.arange(n_blocks, dtype=jnp.int32) < own
            gscore = jnp.where(past, gscore, NEG)
            _, sel = lax.top_k(gscore, n_sel)
            valid = sel < own
            sel_t = jnp.swapaxes(sel, 1, 2)
            k_sel = gather2(kbt, sel_t)
            v_sel = gather2(vbt, sel_t)
            l_sel = jnp.einsum('bqgrd,bgqnsd->bqgrns', qb, k_sel).astype(jnp.float32) * scale
            l_sel = jnp.where(valid[:, :, :, None, :, None], l_sel, NEG)
            n_k = n_sel * MOBA_BLOCK
            logits = jnp.concatenate([l_sel.reshape(B, Q_BLOCK, N_KV_HEADS, GROUP, n_k), l_own], axis=-1)
            p = jax.nn.softmax(logits, axis=-1).astype(v.dtype)
            p_sel = p[..., :n_k].reshape(B, Q_BLOCK, N_KV_HEADS, GROUP, n_sel, MOBA_BLOCK)
            p_own = p[..., n_k:]
            o = (jnp.einsum('bqgrns,bgqnsd->bqgrd', p_sel, v_sel)
                 + jnp.einsum('bqgrs,bsgd->bqgrd', p_own, v_own))
        return o.reshape(B, Q_BLOCK, INNER)

    o = sweep_query_blocks(block, T)
    return (jax.nn.silu(gate) * o) @ w_out


def retention_mixer(h, w_in, gn_g, w_out):
    B, T, _ = h.shape
    pos = jnp.arange(T, dtype=jnp.int32)
    q, k, v, gate = split_cols(h @ w_in, C_WIDTHS)
    q = rope(q.reshape(B, T, RET_HEADS, RET_QK_DIM), pos).astype(jnp.float32)
    k = rope(k.reshape(B, T, RET_HEADS, RET_QK_DIM), pos).astype(jnp.float32) * RET_QK_DIM ** -0.5
    v = v.reshape(B, T, RET_HEADS, RET_V_DIM).astype(jnp.float32)
    log_gamma = jnp.log(1.0 - 2.0 ** (-5.0 - jnp.arange(RET_HEADS, dtype=jnp.float32)))
    i = jnp.arange(RET_CHUNK, dtype=jnp.float32)
    diff = i[:, None] - i[None, :]
    decay_mask = jnp.where(diff >= 0, jnp.exp(jnp.maximum(diff, 0.0)[None] * log_gamma[:, None, None]), 0.0)
    q_decay = jnp.exp((i[:, None] + 1.0) * log_gamma[None, :])
    k_decay = jnp.exp((RET_CHUNK - 1.0 - i[:, None]) * log_gamma[None, :])
    chunk_decay = jnp.exp(RET_CHUNK * log_gamma)
    n_chunks = T // RET_CHUNK

    def to_chunks(a):
        return jnp.swapaxes(a.reshape(B, n_chunks, RET_CHUNK, *a.shape[2:]), 0, 1)

    def step(state, inp):
        qc, kc, vc = inp
        inner = jnp.einsum('bihd,bjhd->bhij', qc, kc) * decay_mask[None]
        o = (jnp.einsum('bhij,bjhv->bihv', inner, vc)
             + jnp.einsum('bihd,bhdv->bihv', qc * q_decay[None, :, :, None], state))
        state = (state * chunk_decay[None, :, None, None]
                 + jnp.einsum('bjhd,bjhv->bhdv', kc * k_decay[None, :, :, None], vc))
        return state, o

    s0 = jnp.zeros((B, RET_HEADS, RET_QK_DIM, RET_V_DIM), jnp.float32)
    _, o = lax.scan(step, s0, (to_chunks(q), to_chunks(k), to_chunks(v)))
    o = jnp.swapaxes(o, 0, 1).reshape(B, T, RET_HEADS, RET_V_DIM)
    mu = jnp.mean(o, axis=-1, keepdims=True)
    var = jnp.mean(jnp.square(o - mu), axis=-1, keepdims=True)
    o = ((o - mu) * lax.rsqrt(var + EPS)).reshape(B, T, RET_INNER) * gn_g.astype(jnp.float32)
    return (jax.nn.silu(gate) * o.astype(h.dtype)) @ w_out


def fox_mixer(h, w_in, f_bias, q_g, k_g, w_out):
    B, T, _ = h.shape
    pos = jnp.arange(T, dtype=jnp.int32)
    q, k, v, gate, f_logit = split_cols(h @ w_in, D_WIDTHS)
    q = rms_norm(q.reshape(B, T, N_HEADS, HEAD_DIM), q_g)
    k = rms_norm(k.reshape(B, T, N_HEADS, HEAD_DIM), k_g)
    v = v.reshape(B, T, N_HEADS, HEAD_DIM)
    log_f = jax.nn.log_sigmoid((f_logit + f_bias).astype(jnp.float32))
    cum_t = jnp.swapaxes(lax.cumsum(log_f, axis=1), 1, 2)
    scale = HEAD_DIM ** -0.5

    def block(start):
        tq = start + jnp.arange(Q_BLOCK, dtype=jnp.int32)
        qb = lax.dynamic_slice_in_dim(q, start, Q_BLOCK, axis=1)
        cq = lax.dynamic_slice_in_dim(cum_t, start, Q_BLOCK, axis=2)
        logits = (jnp.einsum('bqhd,bshd->bhqs', qb, k).astype(jnp.float32) * scale
                  + cq[..., None] - cum_t[:, :, None, :])
        logits = jnp.where((pos[None, :] <= tq[:, None])[None, None], logits, NEG)
        p = jax.nn.softmax(logits, axis=-1).astype(v.dtype)
        o = jnp.einsum('bhqs,bshd->bqhd', p, v)
        return o.reshape(B, Q_BLOCK, INNER)

    o = sweep_query_blocks(block, T)
    return (jax.nn.silu(gate) * o) @ w_out


def setup_inputs(seed: int = 0) -> dict:
    key = jax.random.key(seed)
    ks = jax.random.split(key, 21)
    nA, nB, nC, nD = (n_layers_of(m) for m in range(N_MIXERS))

    def dense(k, shape, fan_in):
        return jax.random.normal(k, shape, jnp.float32) * fan_in ** -0.5

    def gain(k, shape):
        return 1.0 + 0.02 * jax.random.normal(k, shape, jnp.float32)

    return {
        'x': jax.random.normal(ks[0], (BATCH, SEQ, D_MODEL), jnp.float32),
        'a_norm': gain(ks[1], (nA, D_MODEL)),
        'a_w_in': dense(ks[2], (nA, D_MODEL, sum(A_WIDTHS)), D_MODEL),
        'a_q_norm': gain(ks[3], (nA, HEAD_DIM)),
        'a_k_norm': gain(ks[4], (nA, HEAD_DIM)),
        'a_w_out': dense(ks[5], (nA, INNER, D_MODEL), INNER),
        'b_norm': gain(ks[6], (nB, D_MODEL)),
        'b_w_in': dense(ks[7], (nB, D_MODEL, sum(B_WIDTHS)), D_MODEL),
        'b_q_norm': gain(ks[8], (nB, HEAD_DIM)),
        'b_k_norm': gain(ks[9], (nB, HEAD_DIM)),
        'b_w_out': dense(ks[10], (nB, INNER, D_MODEL), INNER),
        'c_norm': gain(ks[11], (nC, D_MODEL)),
        'c_w_in': dense(ks[12], (nC, D_MODEL, sum(C_WIDTHS)), D_MODEL),
        'c_gn': gain(ks[13], (nC, RET_INNER)),
        'c_w_out': dense(ks[14], (nC, RET_INNER, D_MODEL), RET_INNER),
        'd_norm': gain(ks[15], (nD, D_MODEL)),
        'd_w_in': dense(ks[16], (nD, D_MODEL, sum(D_WIDTHS)), D_MODEL),
        'd_f_bias': jax.random.uniform(ks[17], (nD, N_HEADS), jnp.float32, minval=1.0, maxval=4.0),
        'd_q_norm': gain(ks[18], (nD, HEAD_DIM)),
        'd_k_norm': gain(ks[19], (nD, HEAD_DIM)),
        'd_w_out': dense(ks[20], (nD, INNER, D_MODEL), INNER),
    }


def reference(x, a_norm, a_w_in, a_q_norm, a_k_norm, a_w_out,
              b_norm, b_w_in, b_q_norm, b_k_norm, b_w_out,
              c_norm, c_w_in, c_gn, c_w_out,
              d_norm, d_w_in, d_f_bias, d_q_norm, d_k_norm, d_w_out):
    h = x
    for i in range(DEPTH):
        m = i % N_MIXERS
        j = i // N_MIXERS
        if m == 0:
            h = h + dsa_mixer(rms_norm(h, a_norm[j]), a_w_in[j], a_q_norm[j], a_k_norm[j], a_w_out[j])
        elif m == 1:
            h = h + moba_mixer(rms_norm(h, b_norm[j]), b_w_in[j], b_q_norm[j], b_k_norm[j], b_w_out[j])
        elif m == 2:
            h = h + retention_mixer(rms_norm(h, c_norm[j]), c_w_in[j], c_gn[j], c_w_out[j])
        else:
            h = h + fox_mixer(rms_norm(h, d_norm[j]), d_w_in[j], d_f_bias[j], d_q_norm[j], d_k_norm[j], d_w_out[j])
    return h
```

```python
import numpy as np
from contextlib import ExitStack
import concourse.bass as bass
import concourse.mybir as mybir
from concourse.bass_utils import run_bass_kernel_spmd

F32 = mybir.dt.float32
BF16 = mybir.dt.bfloat16
AF = mybir.ActivationFunctionType
ALU = mybir.AluOpType
AX = mybir.AxisListType

T = 2048
D = 2048
NT = 16
KC = 16
EPS = 1e-6
NEGB = -30000.0
ENGS = ('sp', 'act', 'dve', 'pool', 'pe')
NDS = 16
ARENA_WORDS = 52800


class Buf:
    __slots__ = ('name', 'w', 'r')

    def __init__(self, name):
        self.name = name
        self.w = None
        self.r = []


class Prog:
    def __init__(self, nc, ctx):
        self.nc = nc
        self.q = {k: [] for k in ENGS}
        self.esem = {k: ctx.enter_context(nc.semaphore('es_' + k)) for k in ('act', 'dve', 'pool', 'pe')}
        self.ecnt = {k: 0 for k in self.esem}
        self.seen = {k: {} for k in ENGS}
        self.dsem = {k: [ctx.enter_context(nc.semaphore('ds_%s%d' % (k, i))) for i in range(NDS)]
                     for k in ('sp', 'pool', 'act')}
        self.dcnt = {k: 0 for k in self.dsem}
        self.ccsem = ctx.enter_context(nc.semaphore('cc_sem'))
        self.ccnt = 0

    def _collect(self, eng, reads, writes):
        deps = []
        for b in reads:
            if b.w is not None:
                deps.append(b.w)
        for b in writes:
            if b.w is not None:
                deps.append(b.w)
            for ev in b.r:
                if ev[2] is None or ev[2] != eng:
                    deps.append(ev)
        return deps

    def _filter(self, eng, deps):
        need = {}
        for sem, val, peng in deps:
            if peng == 'pe' and eng == 'pe':
                continue
            k = id(sem)
            if self.seen[eng].get(k, 0) >= val:
                continue
            if k not in need or need[k][1] < val:
                need[k] = (sem, val)
        out = []
        for k, (sem, val) in need.items():
            self.seen[eng][k] = val
            out.append((sem, val))
        return out

    def _commit(self, ev, reads, writes):
        for b in reads:
            b.r.append(ev)
        for b in writes:
            b.w = ev
            b.r = []

    def op(self, eng, fn, reads=(), writes=()):
        deps = self._collect(eng, reads, writes)
        waits = self._filter(eng, deps)
        self.ecnt[eng] += 1
        sem = self.esem[eng]
        ev = (sem, self.ecnt[eng], eng)
        self.q[eng].append((waits, fn, (sem, 1)))
        self._commit(ev, reads, writes)
        return ev

    def dma(self, qeng, out, in_, reads=(), writes=(), **kw):
        i = self.dcnt[qeng]
        self.dcnt[qeng] += 1
        slot, gen = i % NDS, i // NDS
        sem = self.dsem[qeng][slot]
        deps = self._collect(None, reads, writes)
        if gen > 0:
            deps.append((sem, 16 * gen, None))
        waits = self._filter(qeng, deps)
        ev = (sem, 16 * (gen + 1), None)
        self.q[qeng].append((waits, (lambda e, out=out, in_=in_, kw=kw: e.dma_start(out=out, in_=in_, **kw)),
                             (sem, 16)))
        self._commit(ev, reads, writes)
        return ev

    def coll(self, fn, reads=(), writes=()):
        deps = self._collect(None, reads, writes)
        waits = self._filter('pool', deps)
        self.ccnt += 1
        ev = (self.ccsem, self.ccnt, None)
        self.q['pool'].append((waits, fn, (self.ccsem, 1)))
        self._commit(ev, reads, writes)
        return ev

    def barrier(self, cc=True):
        if getattr(self, 'flush_hook', None) is not None:
            self.flush_hook()
        evs = []
        if cc and self.ccnt > 0:
            evs.append((self.ccsem, self.ccnt, None))
        for k in self.esem:
            if self.ecnt[k] > 0:
                evs.append((self.esem[k], self.ecnt[k], k))
        for qn in self.dsem:
            n = self.dcnt[qn]
            for slot in range(NDS):
                cnt = (n - slot + NDS - 1) // NDS if n > slot else 0
                if cnt > 0:
                    evs.append((self.dsem[qn][slot], 16 * cnt, None))
        for eng in ENGS:
            mine = [ev for ev in evs if ev[2] is None or ev[2] != eng]
            waits = self._filter(eng, mine)
            if waits:
                self.q[eng].append((waits, None, None))

    def emit(self):
        nc = self.nc
        qs = self.q

        def run(e, ops):
            for waits, fn, inc in ops:
                for sem, val in waits:
                    e.wait_ge(sem, val)
                if fn is not None:
                    ins = fn(e)
                    if inc is not None:
                        ins.then_inc(inc[0], inc[1])

        with nc.Block() as block:
            @block.sync
            def _(e):
                run(e, qs['sp'])

            @block.scalar
            def _(e):
                run(e, qs['act'])

            @block.vector
            def _(e):
                run(e, qs['dve'])

            @block.gpsimd
            def _(e):
                run(e, qs['pool'])

            @block.tensor
            def _(e):
                run(e, qs['pe'])


class Arena:
    def __init__(self, t, nwords):
        self.t = t
        self.nwords = nwords
        self.off = 0
        self.marks = []
        self.peak = 0

    def mark(self):
        self.marks.append(self.off)

    def release(self):
        self.off = self.marks.pop()

    def alloc(self, n, dtype, parts=128):
        words = n if dtype == F32 else (n + 1) // 2
        assert self.off + words <= self.nwords, ('arena overflow', self.off, words, self.nwords)
        ap = self.t[0:parts, self.off:self.off + words]
        self.off += words
        self.peak = max(self.peak, self.off)
        if dtype != F32:
            ap = ap.bitcast(dtype)[:, 0:n]
        return ap


class Ring:
    def __init__(self, A, name, n, size, dtype, parts=128):
        self.aps = [A.alloc(size, dtype, parts) for _ in range(n)]
        self.bufs = [Buf('%s%d' % (name, i)) for i in range(n)]
        self.i = 0

    def next(self):
        k = self.i % len(self.aps)
        self.i += 1
        return self.aps[k], self.bufs[k]


class G:
    pass


def _rot_mat(hd):
    R = np.zeros((128, 128), np.float32)
    half = hd // 2
    for m in range(128):
        b, r = divmod(m, hd)
        if r < half:
            R[b * hd + r + half, m] = -1.0
        else:
            R[b * hd + r - half, m] = 1.0
    return R


def _rope_tab(hd):
    i = (np.arange(128) % (hd // 2)).astype(np.float32)
    inv = (np.float32(10000.0) ** (-(2.0 * i).astype(np.float32) / np.float32(hd))).astype(np.float32)
    pos = np.arange(T, dtype=np.float32)
    ang = (pos[None, :] * inv[:, None]).astype(np.float32)
    return np.stack([np.cos(ang), np.sin(ang)]).astype(np.float32)


def host_consts():
    c = {}
    I = np.eye(128, dtype=np.float32)
    ctri = (np.arange(128)[:, None] > np.arange(128)[None, :]).astype(np.float32)
    c['c_mats'] = np.concatenate([I, NEGB * I, np.ones((128, 128), np.float32), _rot_mat(128), _rot_mat(64), ctri],
                                 axis=1)
    negtri = np.where(np.arange(128)[None, :] > np.arange(128)[:, None], -1e30, 0.0).astype(np.float32)
    misc = np.zeros((128, 8), np.float32)
    misc[:, 0] = D * EPS
    misc[:, 1] = 128 * EPS
    misc[:, 2] = EPS
    c['c_f32'] = np.concatenate([negtri, misc, I], axis=1)
    e8 = np.zeros((8, 8, 128), np.float32)
    for n in range(8):
        e8[n, n, :] = NEGB
    c['c_e8'] = e8.reshape(8, 1024)
    e16 = np.zeros((16, 16, 128), np.float32)
    for n in range(16):
        e16[n, n, :] = 1.0
    c['c_e16'] = e16.reshape(16, 2048)
    e3 = np.zeros((80, 16, 128), np.float32)
    for h in range(16):
        for o in (0, 32, 64):
            e3[o + h, h, :] = -1.0
    c['c_e3'] = e3.reshape(80, 2048)
    c['rope128'] = _rope_tab(128)
    c['rope64'] = _rope_tab(64)
    c['rope256'] = _rope_tab(256)
    lg = np.log(1.0 - 2.0 ** (-5.0 - np.arange(8, dtype=np.float64)))
    s = np.arange(128, dtype=np.float64)[:, None]
    u = np.arange(512, dtype=np.float64)[None, :]
    rw = np.zeros((8, 2, 128, 512), np.float32)
    for h in range(8):
        rw[h, 0] = np.where(u >= s, np.exp(lg[h] * np.maximum(u - s, 0.0)), 0.0)
        rw[h, 1] = np.exp(lg[h] * (u - s + 128.0))
    c['ret_w'] = rw
    c['ret_lg'] = lg
    return c


RET_LG = np.log(1.0 - 2.0 ** (-5.0 - np.arange(8, dtype=np.float64)))


def phase_A1(g, hsrc, norm_ap):
    P, A = g.P, g.A
    A.mark()
    gb = A.alloc(D, F32)
    gbB = Buf('gb')
    P.dma('sp', gb, norm_ap.partition_broadcast(128), writes=[gbB])
    P.op('pool', lambda e: e.tensor_scalar(out=gb, in0=gb, scalar1=float(np.sqrt(float(D))), scalar2=0.0,
                                           op0=ALU.mult, op1=ALU.add), reads=[gbB], writes=[gbB])
    xr = Ring(A, 'xt', 3, D, F32)
    jr = Ring(A, 'jk', 1, D, BF16)
    hr = Ring(A, 'hn', 3, D, BF16)
    sr = Ring(A, 'ssq', 3, 2, F32)
    def stage0(tt):
        xt, xb = xr.next()
        jk, jb = jr.next()
        hn, hb = hr.next()
        ss, sb = sr.next()
        hap, hB_ = hsrc(tt)
        P.dma('sp', xt, hap, reads=[hB_], writes=[xb])
        P.op('act', lambda e: e.activation(out=jk, in_=xt, func=AF.Square, accum_out=ss[:, 0:1]),
             reads=[xb], writes=[jb, sb])
        P.op('act', lambda e: e.activation(out=ss[:, 1:2], in_=ss[:, 0:1], func=AF.Ln, bias=g.c_eps[:, 0:1]),
             reads=[sb], writes=[sb])
        P.op('act', lambda e: e.activation(out=ss[:, 1:2], in_=ss[:, 1:2], func=AF.Exp, scale=-0.5),
             reads=[sb], writes=[sb])
        P.op('dve', lambda e: e.scalar_tensor_tensor(out=hn, in0=xt, scalar=ss[:, 1:2], in1=gb, op0=ALU.mult,
                                                     op1=ALU.mult), reads=[xb, sb, gbB], writes=[hb])
        return hn, hb

    def stage1(tt, hn, hb):
        for half in range(2):
            bk = (tt % 2) * 2 + half
            pb = g.PS[bk].bitcast(BF16)
            for j in range(8):
                kc = half * 8 + j
                P.op('pe', lambda e, pb=pb, j=j, kc=kc: e.transpose(out=pb[:, j * 128:(j + 1) * 128],
                                                                   in_=hn[:, kc * 128:(kc + 1) * 128],
                                                                   identity=g.ident),
                     reads=[hb], writes=[g.PSB[bk]])
            dst = g.bigT[:, half * 8 * T:(half * 8 + 8) * T].rearrange('p (a t) -> p a t', a=8)[:, :, tt * 128:(tt + 1) * 128]
            src = pb.rearrange('p (a b) -> p a b', a=8)
            if half == 0:
                P.op('act', lambda e, dst=dst, src=src: e.copy(out=dst, in_=src), reads=[g.PSB[bk]], writes=[g.bigB])
            else:
                P.op('dve', lambda e, dst=dst, src=src: e.tensor_copy(out=dst, in_=src), reads=[g.PSB[bk]],
                     writes=[g.bigB])

    prev = None
    for tt in range(NT + 1):
        cur = stage0(tt) if tt < NT else None
        if prev is not None:
            stage1(tt - 1, prev[0], prev[1])
        prev = cur
    P.barrier()
    A.release()


def phase_A2(g, w_in, blocks):
    P, A = g.P, g.A
    A.mark()
    wr = Ring(A, 'wbf', 2, KC * 512, BF16)
    sqr = Ring(A, 'sqb', 2, 512, BF16)
    rsr = Ring(A, 'rstd', 2, 512, F32)
    qnr = Ring(A, 'qn', 2, 512, BF16)
    t1r = Ring(A, 't1', 2, 512, F32)
    t2r = Ring(A, 't2', 2, 512, F32)
    obr = Ring(A, 'ob', 4, 512, BF16)
    zi = [0]
    si = [0]
    ri = [0]

    def zbank():
        b = zi[0] % 4
        zi[0] += 1
        return b

    def proj_feat(wv, wB, loff, M, tg, bank):
        for kc in range(KC):
            P.op('pe', lambda e, kc=kc: e.matmul(g.PS[bank][0:M, :], lhsT=wv[:, kc, loff:loff + M],
                                                 rhs=g.bigT[:, kc * T + tg * 512: kc * T + (tg + 1) * 512],
                                                 start=(kc == 0), stop=(kc == KC - 1)),
                 reads=[wB, g.bigB], writes=[g.PSB[bank]])

    def rmsn(bank, M, gcol, gB):
        sq, sqB = sqr.next()
        P.op('act', lambda e: e.activation(out=sq[0:M, :], in_=g.PS[bank][0:M, :], func=AF.Square),
             reads=[g.PSB[bank]], writes=[sqB])
        sb = 4 + (si[0] % 2)
        si[0] += 1
        P.op('pe', lambda e: e.matmul(g.PS[sb][0:M, :], lhsT=g.ones[0:M, 0:M], rhs=sq[0:M, :], start=True, stop=True),
             reads=[sqB], writes=[g.PSB[sb]])
        rs, rsB = rsr.next()
        P.op('act', lambda e: e.activation(out=rs[0:M, :], in_=g.PS[sb][0:M, :], func=AF.Ln, bias=g.c_eps[0:M, 1:2]),
             reads=[g.PSB[sb]], writes=[rsB])
        P.op('act', lambda e: e.activation(out=rs[0:M, :], in_=rs[0:M, :], func=AF.Exp, scale=-0.5),
             reads=[rsB], writes=[rsB])
        qn, qnB = qnr.next()
        P.op('dve', lambda e: e.scalar_tensor_tensor(out=qn[0:M, :], in0=g.PS[bank][0:M, :], scalar=gcol[0:M, 0:1],
                                                     in1=rs[0:M, :], op0=ALU.mult, op1=ALU.mult),
             reads=[g.PSB[bank], rsB, gB], writes=[qnB])
        return qn, qnB

    def rope_mm(qn, qnB, M, rm, cs, csB, tg):
        rb = 6 + (ri[0] % 2)
        ri[0] += 1
        P.op('pe', lambda e: e.matmul(g.PS[rb][0:M, :], lhsT=rm[0:M, 0:M], rhs=qn[0:M, :], start=True, stop=True),
             reads=[qnB], writes=[g.PSB[rb]])
        t1, t1B = t1r.next()
        t2, t2B = t2r.next()
        P.op('pool', lambda e: e.tensor_tensor(out=t1[0:M, :], in0=qn[0:M, :], in1=cs[0][0:M, tg * 512:(tg + 1) * 512],
                                               op=ALU.mult), reads=[qnB, csB], writes=[t1B])
        P.op('dve', lambda e: e.tensor_tensor(out=t2[0:M, :], in0=g.PS[rb][0:M, :],
                                              in1=cs[1][0:M, tg * 512:(tg + 1) * 512], op=ALU.mult),
             reads=[g.PSB[rb], csB], writes=[t2B])
        ob, obB = obr.next()
        P.op('pool', lambda e: e.tensor_tensor(out=ob[0:M, :], in0=t1[0:M, :], in1=t2[0:M, :], op=ALU.add),
             reads=[t1B, t2B], writes=[obB])
        return ob, obB


    tiles = []

    def dma_out_feat(it, M, tg, ob, obB):
        r0 = it['row']
        P.dma('sp', g.zT[r0:r0 + M, tg * 512:(tg + 1) * 512], ob[0:M, :], reads=[obB], writes=[g.zTB])

    def mk_feat(wv, wB, it, tg, pre):
        kind = it['kind']
        M = it['n']
        st = {}

        def s0():
            if pre is not None:
                pre()
            bank = zbank()
            st['bank'] = bank
            proj_feat(wv, wB, it['off'], M, tg, bank)
            if kind == 'qk':
                sq, sqB = sqr.next()
                P.op('act', lambda e: e.activation(out=sq[0:M, :], in_=g.PS[bank][0:M, :], func=AF.Square),
                     reads=[g.PSB[bank]], writes=[sqB])
                st['sq'] = (sq, sqB)

        def s1():
            bank = st['bank']
            if kind == 'qk':
                sq, sqB = st['sq']
                sb = 4 + (si[0] % 2)
                si[0] += 1
                P.op('pe', lambda e: e.matmul(g.PS[sb][0:M, :], lhsT=g.ones[0:M, 0:M], rhs=sq[0:M, :], start=True,
                                              stop=True), reads=[sqB], writes=[g.PSB[sb]])
                rs, rsB = rsr.next()
                P.op('act', lambda e: e.activation(out=rs[0:M, :], in_=g.PS[sb][0:M, :], func=AF.Ln,
                                                   bias=g.c_eps[0:M, 1:2]), reads=[g.PSB[sb]], writes=[rsB])
                P.op('act', lambda e: e.activation(out=rs[0:M, :], in_=rs[0:M, :], func=AF.Exp, scale=-0.5),
                     reads=[rsB], writes=[rsB])
                qn, qnB = qnr.next()
                gcol, gB = it['gcol'], it['gB']
                P.op('dve', lambda e: e.scalar_tensor_tensor(out=qn[0:M, :], in0=g.PS[bank][0:M, :],
                                                             scalar=gcol[0:M, 0:1], in1=rs[0:M, :], op0=ALU.mult,
                                                             op1=ALU.mult),
                     reads=[g.PSB[bank], rsB, gB], writes=[qnB])
                st['qn'] = (qn, qnB)
                if it.get('rope') is None:
                    dma_out_feat(it, M, tg, qn, qnB)
            elif kind == 'rope':
                qn, qnB = qnr.next()
                P.op('act', lambda e: e.activation(out=qn[0:M, :], in_=g.PS[bank][0:M, :], func=AF.Copy,
                                                   scale=float(it.get('scale', 1.0))),
                     reads=[g.PSB[bank]], writes=[qnB])
                st['qn'] = (qn, qnB)
            elif kind == 'gate':
                ob, obB = obr.next()
                P.op('act', lambda e: e.activation(out=ob[0:M, :], in_=g.PS[bank][0:M, :], func=AF.Silu),
                     reads=[g.PSB[bank]], writes=[obB])
                dma_out_feat(it, M, tg, ob, obB)
            elif kind == 'fox_f':
                dst, dB = it['sb_dst']
                fb = it['fbias']
                P.op('act', lambda e: e.activation(out=dst[0:M, tg * 512:(tg + 1) * 512], in_=g.PS[bank][0:M, :],
                                                   func=AF.Exp, scale=-1.0, bias=fb[0:M, 0:1]),
                     reads=[g.PSB[bank], it['fbB']], writes=[dB])
                P.op('act', lambda e: e.activation(out=dst[0:M, tg * 512:(tg + 1) * 512],
                                                   in_=dst[0:M, tg * 512:(tg + 1) * 512], func=AF.Ln,
                                                   bias=g.c_one[0:M, 0:1]), reads=[dB], writes=[dB])
            else:
                raise ValueError(kind)

        def s2():
            if it.get('rope') is not None and kind in ('qk', 'rope'):
                qn, qnB = st['qn']
                rm, cs, csB = it['rope']
                ob, obB = rope_mm(qn, qnB, M, rm, cs, csB, tg)
                dma_out_feat(it, M, tg, ob, obB)

        return [s0, s1, s2]

    def mk_pair(wv, wB, it, tg, pre):
        cs, csB = it['cs']
        sc = float(it.get('scale', 1.0))
        st = {}
        cosv = cs[0][:, tg * 512:(tg + 1) * 512]
        sinv = cs[1][:, tg * 512:(tg + 1) * 512]

        def s0():
            if pre is not None:
                pre()
            b0 = zbank()
            proj_feat(wv, wB, it['off'], 128, tg, b0)
            b1 = zbank()
            proj_feat(wv, wB, it['off'] + 128, 128, tg, b1)
            st['b'] = (b0, b1)

        def s1():
            b0, b1 = st['b']
            x1, x1B = t1r.next()
            x2, x2B = t2r.next()
            P.op('act', lambda e: e.activation(out=x1, in_=g.PS[b0], func=AF.Copy, scale=sc), reads=[g.PSB[b0]],
                 writes=[x1B])
            P.op('act', lambda e: e.activation(out=x2, in_=g.PS[b1], func=AF.Copy, scale=sc), reads=[g.PSB[b1]],
                 writes=[x2B])
            st['x'] = (x1, x1B, x2, x2B)

        def s2():
            x1, x1B, x2, x2B = st['x']
            a1, a1B = rsr.next()
            a2, a2B = rsr.next()
            P.op('dve', lambda e: e.tensor_tensor(out=a1, in0=x1, in1=cosv, op=ALU.mult), reads=[x1B, csB], writes=[a1B])
            P.op('pool', lambda e: e.tensor_tensor(out=a2, in0=x2, in1=sinv, op=ALU.mult), reads=[x2B, csB], writes=[a2B])
            o1, o1B = obr.next()
            P.op('dve', lambda e: e.tensor_tensor(out=o1, in0=a1, in1=a2, op=ALU.subtract), reads=[a1B, a2B],
                 writes=[o1B])
            P.op('pool', lambda e: e.tensor_tensor(out=a1, in0=x1, in1=sinv, op=ALU.mult), reads=[x1B, csB, o1B],
                 writes=[a1B])
            P.op('dve', lambda e: e.tensor_tensor(out=a2, in0=x2, in1=cosv, op=ALU.mult), reads=[x2B, csB, o1B],
                 writes=[a2B])
            o2, o2B = obr.next()
            P.op('pool', lambda e: e.tensor_tensor(out=o2, in0=a1, in1=a2, op=ALU.add), reads=[a1B, a2B], writes=[o2B])
            r0 = it['row']
            P.dma('sp', g.zT[r0:r0 + 128, tg * 512:(tg + 1) * 512], o1, reads=[o1B], writes=[g.zTB])
            P.dma('sp', g.zT[r0 + 128:r0 + 256, tg * 512:(tg + 1) * 512], o2, reads=[o2B], writes=[g.zTB])

        return [s0, s1, s2]

    def mk_tok(wv, wB, it, tt, pre):
        n = it['n']
        st = {}

        def s0():
            if pre is not None:
                pre()
            bank = zbank()
            st['bank'] = bank
            for kc in range(KC):
                P.op('pe', lambda e, kc=kc: e.matmul(g.PS[bank][:, 0:n],
                                                     lhsT=g.bigT[:, kc * T + tt * 128: kc * T + (tt + 1) * 128],
                                                     rhs=wv[:, kc, it['off']:it['off'] + n], start=(kc == 0),
                                                     stop=(kc == KC - 1)),
                     reads=[wB, g.bigB], writes=[g.PSB[bank]])

        def s1():
            bank = st['bank']
            if it.get('sb_dst') is not None:
                dst, dB = it['sb_dst']
                P.op('act', lambda e: e.copy(out=dst[:, tt * n:(tt + 1) * n], in_=g.PS[bank][:, 0:n]),
                     reads=[g.PSB[bank]], writes=[dB])
            else:
                ob, obB = obr.next()
                P.op('act', lambda e: e.copy(out=ob[:, 0:n], in_=g.PS[bank][:, 0:n]), reads=[g.PSB[bank]], writes=[obB])
                vc = it['vcol']
                P.dma('sp', g.vtok[tt * 128:(tt + 1) * 128, vc:vc + n], ob[:, 0:n], reads=[obB], writes=[g.vtB])

        return [s0, s1, None]

    pres = []
    for bi, blk in enumerate(blocks):
        col0, ncols, items = blk[0], blk[1], blk[2]
        loader = blk[3] if len(blk) > 3 else None
        holder = {}

        def pre(col0=col0, ncols=ncols, loader=loader, holder=holder):
            wbf, wB = wr.next()
            wv = wbf.rearrange('p (k n) -> p k n', k=KC)
            holder['wv'], holder['wB'] = wv, wB
            if loader is not None:
                loader(wv, wB)
            else:
                P.dma('pool', wv[:, :, 0:ncols], w_in[:, col0:col0 + ncols].rearrange('(k p) n -> p k n', p=128),
                      writes=[wB])

        pres.append(pre)
        first = True
        for it in items:
            reps = NT if it['kind'] == 'tok' else 4
            for r in range(reps):
                def lazy(it=it, r=r, holder=holder, pre_fn=((lambda bi=bi: pres[bi + 1]() if bi + 1 < len(pres) else None) if first else None)):
                    built = {}

                    def s0():
                        if pre_fn is not None:
                            pre_fn()
                        wv, wB = holder['wv'], holder['wB']
                        if it['kind'] == 'tok':
                            built['s'] = mk_tok(wv, wB, it, r, None)
                        elif it['kind'] == 'rope_pair':
                            built['s'] = mk_pair(wv, wB, it, r, None)
                        else:
                            built['s'] = mk_feat(wv, wB, it, r, None)
                        built['s'][0]()

                    def s1():
                        built['s'][1]()

                    def s2():
                        if built['s'][2] is not None:
                            built['s'][2]()

                    return [s0, s1, s2]
                tiles.append(lazy())
                first = False
    pres[0]()
    nt_ = len(tiles)
    for step in range(nt_ + 2):
        if step < nt_:
            tiles[step][0]()
        if 0 <= step - 1 < nt_:
            tiles[step - 1][1]()
        if 0 <= step - 2 < nt_:
            tiles[step - 2][2]()

    P.barrier()
    A.release()


def flush_att(g):
    pend = getattr(g, 'att_pending', None)
    if pend is not None:
        g.att_pending = None
        pend()


def attend(g, N, qch, qB, ktiles, n_dv, Sb, Ob, Lb, ptr, onesL=True, finish=None):
    P = g.P
    n = len(ktiles)
    sbase = getattr(g, 'scnt', 0)
    g.scnt = sbase + n

    def emit_S(i):
        kt = ktiles[i]
        bank = Sb[(sbase + i) % 2]
        c0 = kt['c0']
        mms = []
        for kc, (kl, qc) in enumerate(zip(kt['k'], qch)):
            mms.append((kl, qc[:, c0:N], c0, N, list(kt['kB']) + list(qB)))
        for (bl, br, lo, hi, bb) in kt.get('bias', []):
            mms.append((bl, br, lo, hi, list(bb)))
        for j, (lt, rh, lo, hi, bb) in enumerate(mms):
            P.op('pe', lambda e, lt=lt, rh=rh, lo=lo, hi=hi, j=j, bank=bank, last=(j == len(mms) - 1): e.matmul(
                g.PS[bank][:, lo:hi], lhsT=lt, rhs=rh, start=(j == 0), stop=last),
                 reads=bb, writes=[g.PSB[bank]])

    def emit_P(i):
        kt = ktiles[i]
        bank = Sb[(sbase + i) % 2]
        c0 = kt['c0']
        pt, ptB = ptr.next()
        eng, fn, rd = kt['pt'](g.PS[bank][:, c0:N], pt[:, c0:N], c0)
        P.op(eng, fn, reads=[g.PSB[bank]] + list(rd), writes=[ptB])
        return pt, ptB

    def emit_PV(i, pt, ptB):
        kt = ktiles[i]
        c0 = kt['c0']
        for j in range(n_dv):
            P.op('pe', lambda e, j=j, c0=c0, pt=pt, kt=kt: e.matmul(g.PS[Ob[j]][:, c0:N], lhsT=kt['v'][j],
                                                                     rhs=pt[:, c0:N], start=(i == 0), stop=(i == n - 1)),
                 reads=[ptB] + list(kt['vB']), writes=[g.PSB[Ob[j]]])
        if onesL:
            P.op('pe', lambda e, c0=c0, pt=pt: e.matmul(g.PS[Lb][:, c0:N], lhsT=g.ones, rhs=pt[:, c0:N],
                                                        start=(i == 0), stop=(i == n - 1)),
                 reads=[ptB], writes=[g.PSB[Lb]])

    emit_S(0)
    first = emit_P(0)
    flush_att(g)
    for i in range(n):
        pt, ptB = first if i == 0 else emit_P(i)
        if i + 1 < n:
            emit_S(i + 1)
        if i < n - 1:
            emit_PV(i, pt, ptB)
        else:
            def tail(i=i, pt=pt, ptB=ptB):
                emit_PV(i, pt, ptB)
                if finish is not None:
                    finish()
            g.att_pending = tail


def pt_exp(bias_ap=None, biasB=()):
    def f(S, PTv, c0):
        if bias_ap is None:
            return 'act', (lambda e: e.activation(out=PTv, in_=S, func=AF.Exp)), []
        return 'act', (lambda e: e.activation(out=PTv, in_=S, func=AF.Exp, bias=bias_ap)), list(biasB)
    return f


def softmax_finish(g, N, Obank, Lbank, gate_ap, gateB, dst_ap, rlr, tmr):
    P = g.P
    rl, rlB = rlr.next()
    P.op('dve', lambda e: e.reciprocal(out=rl[:, 0:N], in_=g.PS[Lbank][:, 0:N]), reads=[g.PSB[Lbank]], writes=[rlB])
    tm, tmB = tmr.next()
    P.op('dve', lambda e: e.tensor_tensor(out=tm[:, 0:N], in0=g.PS[Obank][:, 0:N], in1=rl[:, 0:N], op=ALU.mult),
         reads=[g.PSB[Obank], rlB], writes=[tmB])
    P.op('pool', lambda e: e.tensor_tensor(out=dst_ap, in0=tm[:, 0:N], in1=gate_ap, op=ALU.mult),
         reads=[tmB] + list(gateB), writes=[g.bigB])


def phase_C(g, w_out, nchunks, h_src, hbufs_src, h_dst, hbufs_dst, y_prev=None, y_out=None):
    P, A = g.P, g.A
    A.mark()
    wr = Ring(A, 'wo', 2, nchunks * 512, BF16)
    hr = Ring(A, 'hres', 3, 512, F32)
    yr = Ring(A, 'yprev', 2, 512, F32)
    zi = 0
    for cg in range(4):
        wbf, wB = wr.next()
        wv = wbf.rearrange('p (k n) -> p k n', k=nchunks)
        P.dma('pool', wv, w_out[:, cg * 512:(cg + 1) * 512].rearrange('(k p) n -> p k n', p=128), writes=[wB])
        for tt in range(NT):
            bank = zi % 4
            zi += 1
            ht, hB = hr.next()
            if y_out is None:
                P.dma('sp', ht, h_src[tt * 128:(tt + 1) * 128, cg * 512:(cg + 1) * 512], reads=[hbufs_src[tt]],
                      writes=[hB])
            if y_prev is not None:
                yp, ypB = yr.next()
                P.dma('sp', yp, y_prev[tt * 128:(tt + 1) * 128, cg * 512:(cg + 1) * 512], reads=[g.ypB], writes=[ypB])
            for c in range(nchunks):
                P.op('pe', lambda e, c=c, bank=bank, tt=tt, wv=wv: e.matmul(
                    g.PS[bank], lhsT=g.bigT[:, c * T + tt * 128: c * T + (tt + 1) * 128], rhs=wv[:, c, :],
                    start=(c == 0), stop=(c == nchunks - 1)), reads=[wB, g.bigB], writes=[g.PSB[bank]])
            if y_out is not None:
                P.op('act', lambda e, ht=ht, bank=bank: e.copy(out=ht, in_=g.PS[bank]), reads=[g.PSB[bank]], writes=[hB])
                P.dma('sp', y_out[tt * 128:(tt + 1) * 128, cg * 512:(cg + 1) * 512], ht, reads=[hB], writes=[g.ypB])
            else:
                P.op('dve', lambda e, ht=ht, bank=bank: e.tensor_tensor(out=ht, in0=g.PS[bank], in1=ht, op=ALU.add),
                     reads=[g.PSB[bank], hB], writes=[hB])
                if y_prev is not None:
                    P.op('pool', lambda e, ht=ht, yp=yp: e.tensor_tensor(out=ht, in0=ht, in1=yp, op=ALU.add),
                         reads=[hB, ypB], writes=[hB])
                P.dma('sp', h_dst[tt * 128:(tt + 1) * 128, cg * 512:(cg + 1) * 512], ht, reads=[hB],
                      writes=[hbufs_dst[tt]])
    P.barrier()
    A.release()


def load_col(g, A, src_row_ap, n, mult, name):
    P = g.P
    col = A.alloc(1, F32)
    B = Buf(name)
    P.dma('sp', col[0:n, :], src_row_ap.rearrange('o d -> d o'), writes=[B])
    P.op('pool', lambda e: e.tensor_scalar(out=col[0:n, :], in0=col[0:n, :], scalar1=float(mult), scalar2=None,
                                           op0=ALU.mult), reads=[B], writes=[B])
    return col, B


def load_rope(g, A, tab, name):
    P = g.P
    cs = [A.alloc(T, F32), A.alloc(T, F32)]
    B = Buf(name)
    P.dma('sp', cs[0], tab[0], writes=[B])
    P.dma('sp', cs[1], tab[1], writes=[B])
    return cs, B


def load_kv(g, A, krow0, nkc, vcol0, ndv, kdst, kB, vdst, vB, nkt=NT):
    P = g.P
    nk = nkt * 128
    P.dma('sp', kdst.rearrange('p (c t) -> p c t', c=nkc)[:, :, 0:nk],
          g.zT[krow0:krow0 + nkc * 128, 0:nk].rearrange('(c p) t -> p c t', p=128), reads=[g.zTB], writes=[kB])
    P.dma('sp', vdst.rearrange('p (k d) -> p k d', k=NT)[:, 0:nkt, :],
          g.vtok[0:nk, vcol0:vcol0 + ndv * 128].rearrange('(k p) d -> p k d', p=128), reads=[g.vtB], writes=[vB])


def softmax_finish_act(g, N, Obank, Lbank, gate_ap, gateB, dst_ap, rlr, tmr):
    P = g.P
    tm, tmB = tmr.next()
    P.op('act', lambda e: e.copy(out=tm[:, 0:N], in_=g.PS[Obank][:, 0:N]), reads=[g.PSB[Obank]], writes=[tmB])
    rl, rlB = rlr.next()
    P.op('act', lambda e: e.activation(out=rl[:, 0:N], in_=g.PS[Lbank][:, 0:N], func=AF.Ln), reads=[g.PSB[Lbank]],
         writes=[rlB])
    P.op('act', lambda e: e.activation(out=rl[:, 0:N], in_=rl[:, 0:N], func=AF.Exp, scale=-1.0), reads=[rlB],
         writes=[rlB])
    P.op('pool', lambda e: e.tensor_tensor(out=tm[:, 0:N], in0=tm[:, 0:N], in1=rl[:, 0:N], op=ALU.mult),
         reads=[tmB, rlB], writes=[tmB])
    P.op('pool', lambda e: e.tensor_tensor(out=dst_ap, in0=tm[:, 0:N], in1=gate_ap, op=ALU.mult),
         reads=[tmB] + list(gateB), writes=[g.bigB])


TP = 2
NH = 16 // TP
NKV = 4 // TP
RH = 8 // TP
GROUPS = [[0, 1], [2, 3], [4, 5], [6, 7]]


class OutProj:
    def __init__(self, g, w_out, nchunks, banks, hsrc, oset, pool_path=False, nbuf=2, scratch=None):
        P, A = g.P, g.A
        self.g, self.nchunks, self.banks = g, nchunks, banks
        self.hsrc, self.oset, self.pool_path = hsrc, oset, pool_path
        self.tmr = Ring(A, 'ctmp', 2, 512, F32) if pool_path else None
        wo = A.alloc(nchunks * D, BF16)
        self.wv = wo.rearrange('p (k n) -> p k n', k=nchunks)
        self.woB = [Buf('wo%d' % i) for i in range(4)]
        for cg in range(4):
            P.dma('pool', self.wv[:, :, cg * 512:(cg + 1) * 512],
                  w_out[:, cg * 512:(cg + 1) * 512].rearrange('(k p) n -> p k n', p=128), writes=[self.woB[cg]])
        if scratch is not None:
            nb = scratch.shape[1] // D
            self.yr = Ring.__new__(Ring)
            self.yr.aps = [scratch[:, i * D:(i + 1) * D] for i in range(nb)]
            self.yr.bufs = [Buf('ytS%d' % i) for i in range(nb)]
            self.yr.i = 0
        else:
            self.yr = Ring(A, 'yt', nbuf, D, F32)
        self.zi = 0
        self.pref = {}

    def prefetch(self, j):
        if len(self.yr.aps) < 4 or j in self.pref:
            return
        P = self.g.P
        tiles = []
        for tt in range(4 * j, 4 * j + 4):
            yt, ytB = self.yr.next()
            hap, hB_ = self.hsrc(tt)
            P.dma('sp', yt, hap, reads=[hB_], writes=[ytB])
            tiles.append((yt, ytB))
        self.pref[j] = tiles

    def chunk(self, j):
        g = self.g
        P = g.P
        flush_att(g)
        nchunks, wv, woB = self.nchunks, self.wv, self.woB
        oset = self.oset
        pre = len(self.yr.aps) >= 4
        if pre and j not in self.pref:
            self.prefetch(j)
        tiles = self.pref.pop(j) if pre else []
        for ti, tt in enumerate(range(4 * j, 4 * j + 4)):
            if pre:
                yt, ytB = tiles[ti]
            else:
                yt, ytB = self.yr.next()
                hap, hB_ = self.hsrc(tt)
                P.dma('sp', yt, hap, reads=[hB_], writes=[ytB])
            if self.pool_path:
                P.op('pool', lambda e, yt=yt: e.tensor_scalar(out=yt, in0=yt, scalar1=0.5, scalar2=None, op0=ALU.mult),
                     reads=[ytB], writes=[ytB])
            for cg in range(4):
                bank = self.banks[self.zi % len(self.banks)]
                self.zi += 1
                for c in range(nchunks):
                    P.op('pe', lambda e, c=c, bank=bank, tt=tt, cg=cg: e.matmul(
                        g.PS[bank], lhsT=g.bigT[:, c * T + tt * 128: c * T + (tt + 1) * 128],
                        rhs=wv[:, c, cg * 512:(cg + 1) * 512], start=(c == 0), stop=(c == nchunks - 1)),
                         reads=[woB[cg], g.bigB], writes=[g.PSB[bank]])
                ysl = yt[:, cg * 512:(cg + 1) * 512]
                if self.pool_path:
                    tm, tmB = self.tmr.next()
                    P.op('act', lambda e, tm=tm, bank=bank: e.copy(out=tm, in_=g.PS[bank]), reads=[g.PSB[bank]],
                         writes=[tmB])
                    P.op('pool', lambda e, ysl=ysl, tm=tm: e.tensor_tensor(out=ysl, in0=ysl, in1=tm, op=ALU.add),
                         reads=[tmB, ytB], writes=[ytB])
                else:
                    P.op('dve', lambda e, ysl=ysl, bank=bank: e.scalar_tensor_tensor(
                        out=ysl, in0=ysl, scalar=0.5, in1=g.PS[bank], op0=ALU.mult, op1=ALU.add),
                         reads=[g.PSB[bank], ytB], writes=[ytB])
            P.dma('pool', g.ypart[j][(tt % 4) * 128:(tt % 4 + 1) * 128, :], yt, reads=[ytB], writes=[g.ypB[j]])
        P.coll(lambda e, j=j: e.collective_compute("AllReduce", ALU.add, replica_groups=GROUPS,
                                                   ins=[g.ypart_t[j].ap().opt()],
                                                   outs=[g.ysum_t[oset][j].ap().opt()]),
               reads=[g.ypB[j]], writes=[g.ysB[oset][j]])


def final_copy(g, oset, out, obufs):
    P = g.P
    for j in range(4):
        P.dma('sp', out[j * 512:(j + 1) * 512, :], g.ysum[oset][j], reads=[g.ysB[oset][j]], writes=[obufs[j]])


QC0, KC0, VC0, GC0 = 0, NH * 128, NH * 128 + NKV * 128, NH * 128 + 2 * NKV * 128
IC0 = GC0 + NH * 128
KR0, GR0 = NH * 128, NH * 128 + NKV * 128
IR0 = GR0 + NH * 128


def gqa_blocks(gq, gqB, gk, gkB, rope):
    blocks = []
    for b in range(NH // 4):
        blocks.append((QC0 + b * 512, 512, [dict(kind='qk', off=j * 128, n=128, row=(b * 4 + j) * 128, gcol=gq, gB=gqB,
                                                 rope=rope) for j in range(4)]))
    blocks.append((KC0, NKV * 128, [dict(kind='qk', off=j * 128, n=128, row=KR0 + j * 128, gcol=gk, gB=gkB, rope=rope)
                                    for j in range(NKV)]))
    blocks.append((VC0, NKV * 128, [dict(kind='tok', off=0, n=NKV * 128, vcol=0)]))
    for b in range(NH // 4):
        blocks.append((GC0 + b * 512, 512, [dict(kind='gate', off=j * 128, n=128, row=GR0 + (b * 4 + j) * 128)
                                            for j in range(4)]))
    return blocks


def gqa_attention_tile(g, qt, gi, kT, kB, vS, vB, q4r, g4r, ptr, rlr, tmr, bias_fn, Ob, Lb, pre_fn=None):
    P = g.P
    q4, q4B = q4r.next()
    g4, g4B = g4r.next()
    P.dma('sp', q4.rearrange('p (r q) -> p r q', r=4),
          g.zT[gi * 512:(gi + 1) * 512, qt * 128:(qt + 1) * 128].rearrange('(r p) q -> p r q', p=128),
          reads=[g.zTB], writes=[q4B])
    P.dma('sp', g4.rearrange('p (r q) -> p r q', r=4),
          g.zT[GR0 + gi * 512:GR0 + (gi + 1) * 512, qt * 128:(qt + 1) * 128].rearrange('(r p) q -> p r q', p=128),
          reads=[g.zTB], writes=[g4B])
    if pre_fn is not None:
        pre_fn(q4, q4B)
    kts = []
    for kt in range(qt + 1):
        bias = []
        for (bl, br, bb) in bias_fn(kt):
            for r in range(4):
                bias.append((bl, br, r * 128, (r + 1) * 128, bb))
        kts.append(dict(k=[kT[:, gi * T + kt * 128: gi * T + (kt + 1) * 128]], kB=[kB],
                        v=[vS[:, kt * NKV * 128 + gi * 128: kt * NKV * 128 + (gi + 1) * 128]], vB=[vB], c0=0, bias=bias,
                        pt=pt_exp()))
    dst = g.bigT[:, gi * 4 * T:(gi * 4 + 4) * T].rearrange('p (r t) -> p r t', r=4)[:, :, qt * 128:(qt + 1) * 128]
    tmv = lambda ap: ap.rearrange('p (r q) -> p r q', r=4)

    def fin():
        tm, tmB = tmr.next()
        P.op('act', lambda e: e.copy(out=tm, in_=g.PS[Ob]), reads=[g.PSB[Ob]], writes=[tmB])
        rl, rlB = rlr.next()
        P.op('act', lambda e: e.activation(out=rl, in_=g.PS[Lb], func=AF.Ln), reads=[g.PSB[Lb]], writes=[rlB])
        P.op('act', lambda e: e.activation(out=rl, in_=rl, func=AF.Exp, scale=-1.0), reads=[rlB], writes=[rlB])
        P.op('pool', lambda e: e.tensor_tensor(out=tm, in0=tm, in1=rl, op=ALU.mult), reads=[tmB, rlB], writes=[tmB])
        P.op('pool', lambda e: e.tensor_tensor(out=dst, in0=tmv(tm), in1=tmv(g4), op=ALU.mult), reads=[tmB, g4B],
             writes=[g.bigB])

    attend(g, 512, [q4], [q4B], kts, 1, [0, 1], [Ob], Lb, ptr, finish=fin)


def layer_dsa(g, prm):
    P, A = g.P, g.A
    A.mark()
    wi_sb = A.alloc(NT * 16, F32)
    wiB = Buf('wi')
    A.mark()
    cs128, cs128B = load_rope(g, A, g.rope128, 'r128')
    cs64, cs64B = load_rope(g, A, g.rope64, 'r64')
    gq, gqB = load_col(g, A, prm['qn'], 128, 1.0, 'gq')
    gk, gkB = load_col(g, A, prm['kn'], 128, float(np.sqrt(128.0)), 'gk')
    blocks = gqa_blocks(gq, gqB, gk, gkB, (g.rm128, cs128, cs128B))
    for b in range(2):
        blocks.append((IC0 + b * 512, 512, [dict(kind='rope', off=j * 128, n=128, row=IR0 + (b * 4 + j) * 128,
                                                 rope=(g.rm64, cs64, cs64B)) for j in range(4)]))
    blocks.append((IC0 + 1024, 80, [dict(kind='rope', off=0, n=64, row=IR0 + 1024, rope=(g.rm64, cs64, cs64B)),
                                    dict(kind='tok', off=64, n=16, sb_dst=(wi_sb, wiB))]))
    phase_A2(g, prm['w_in'], blocks)
    A.release()
    A.mark()
    kT = A.alloc(NKV * T, BF16)
    vS = A.alloc(NKV * NT * 128, BF16)
    ki2 = A.alloc(T, BF16)
    kB, vB, kiB = Buf('kT'), Buf('vS'), Buf('ki2')
    load_kv(g, A, KR0, NKV, 0, NKV, kT, kB, vS, vB)
    P.dma('sp', ki2[0:64, :], g.zT[IR0 + 1024:IR0 + 1088, :], reads=[g.zTB], writes=[kiB])
    P.dma('sp', ki2[64:128, :], g.zT[IR0 + 1024:IR0 + 1088, :], reads=[g.zTB], writes=[kiB])
    q4r = Ring(A, 'q4', 2, 512, BF16)
    g4r = Ring(A, 'g4', 2, 512, BF16)
    qir = Ring(A, 'qi', 2, 8 * 128, BF16)
    accr = Ring(A, 'acc', 2, T, F32)
    rrr = Ring(A, 'rr', 3, 512, BF16)
    dgr = Ring(A, 'dg', 4, 128, BF16)
    jkr = Ring(A, 'junk', 2, T, BF16)
    nsr = Ring(A, 'ns', 2, T, BF16)
    mtr = Ring(A, 'maskT', 4, T, BF16)
    ptr = Ring(A, 'pt', 3, 512, BF16)
    rlr = Ring(A, 'rl', 2, 512, F32)
    tmr = Ring(A, 'tm', 2, 512, F32)
    smr = Ring(A, 'bis', 2, 40, F32)
    oproj = OutProj(g, prm['w_out'], NH, [4, 5, 6, 7], prm['hsrc'], prm['oset'], scratch=g.bigT[:, NH * T:KC * T].bitcast(F32))
    p2 = A.alloc(32, F32)
    p2B = Buf('p2')
    NBIS = 16
    for i in range(NBIS):
        P.op('pool', lambda e, i=i: e.memset(p2[:, i:i + 1], float(2.0 ** (-i))), writes=[p2B])
    masks = {}

    def indexer_pair(qts):
        mem = []
        for qt in qts:
            L = (qt + 1) * 128
            qi, qiB = qir.next()
            P.dma('sp', qi.rearrange('p (a q) -> p a q', a=8),
                  g.zT[IR0:IR0 + 1024, qt * 128:(qt + 1) * 128].rearrange('(a p) q -> p a q', p=128), reads=[g.zTB],
                  writes=[qiB])
            acc, accB = accr.next()
            nch = (L + 511) // 512
            cnt = 0
            for c in range(nch):
                ncol = min(512, L - c * 512)
                pend = None
                for h in range(17):
                    if h < 16:
                        pair, half = divmod(h, 2)
                        rb = 4 + (cnt % 2)
                        cnt += 1
                        P.op('pe', lambda e, rb=rb, ncol=ncol, half=half, pair=pair, c=c, qi=qi: e.matmul(
                            g.PS[rb][:, 0:ncol], lhsT=qi[half * 64:(half + 1) * 64, pair * 128:(pair + 1) * 128],
                            rhs=ki2[half * 64:(half + 1) * 64, c * 512:c * 512 + ncol], start=True, stop=True),
                             reads=[qiB, kiB], writes=[g.PSB[rb]])
                        rr, rrB = rrr.next()
                        P.op('act', lambda e, rr=rr, rb=rb, ncol=ncol: e.activation(out=rr[:, 0:ncol],
                                                                                   in_=g.PS[rb][:, 0:ncol],
                                                                                   func=AF.Relu),
                             reads=[g.PSB[rb]], writes=[rrB])
                        dg, dgB = dgr.next()
                        wcol = wi_sb[:, qt * 16 + h: qt * 16 + h + 1]
                        P.op('dve', lambda e, dg=dg, wcol=wcol: e.tensor_scalar(out=dg, in0=g.ident, scalar1=wcol,
                                                                                scalar2=None, op0=ALU.mult),
                             reads=[wiB], writes=[dgB])
                        cur = (dg, dgB, rr, rrB, h)
                    else:
                        cur = None
                    if pend is not None:
                        pdg, pdgB, prr, prrB, ph = pend
                        P.op('pe', lambda e, pdg=pdg, prr=prr, ncol=ncol, ph=ph: e.matmul(
                            g.PS[6][:, 0:ncol], lhsT=pdg, rhs=prr[:, 0:ncol], start=(ph == 0), stop=(ph == 15)),
                             reads=[pdgB, prrB], writes=[g.PSB[6]])
                    pend = cur
                P.op('act', lambda e, acc=acc, c=c, ncol=ncol: e.copy(out=acc[:, c * 512:c * 512 + ncol],
                                                                      in_=g.PS[6][:, 0:ncol]),
                     reads=[g.PSB[6]], writes=[accB])
            sm, smB = smr.next()
            jk, jkB = jkr.next()
            mem.append(dict(qt=qt, L=L, acc=acc, accB=accB, sm=sm, smB=smB, jk=jk, jkB=jkB))
        for m in mem:
            sm, smB, acc, accB, L, qt = m['sm'], m['smB'], m['acc'], m['accB'], m['L'], m['qt']
            P.op('dve', lambda e, sm=sm, acc=acc, L=L: e.tensor_reduce(out=sm[:, 0:1], in_=acc[:, 0:L], axis=AX.X,
                                                                       op=ALU.max, apply_absolute_value=True),
                 reads=[accB], writes=[smB])
            P.op('dve', lambda e, sm=sm: e.tensor_scalar(out=sm[:, 0:1], in0=sm[:, 0:1], scalar1=1.0001, scalar2=1e-6,
                                                         op0=ALU.mult, op1=ALU.add), reads=[smB], writes=[smB])
            P.op('dve', lambda e, sm=sm: e.tensor_scalar(out=sm[:, 1:2], in0=sm[:, 0:1], scalar1=-1.0, scalar2=None,
                                                         op0=ALU.mult), reads=[smB], writes=[smB])
            P.op('dve', lambda e, sm=sm: e.tensor_scalar(out=sm[:, 8:8 + NBIS], in0=p2[:, 0:NBIS], scalar1=sm[:, 0:1],
                                                         scalar2=None, op0=ALU.mult), reads=[smB, p2B], writes=[smB])
            P.op('dve', lambda e, acc=acc, qt=qt: e.tensor_tensor(out=acc[:, qt * 128:(qt + 1) * 128],
                                                                  in0=acc[:, qt * 128:(qt + 1) * 128], in1=g.negtri,
                                                                  op=ALU.add), reads=[accB], writes=[accB])
        for i in range(NBIS):
            for m in mem:
                sm, smB = m['sm'], m['smB']
                P.op('dve', lambda e, sm=sm, i=i: e.tensor_tensor(out=sm[:, 2:3], in0=sm[:, 1:2], in1=sm[:, 8 + i:9 + i],
                                                                  op=ALU.add), reads=[smB], writes=[smB])
            for m in mem:
                sm, smB, acc, accB, L, jk, jkB = m['sm'], m['smB'], m['acc'], m['accB'], m['L'], m['jk'], m['jkB']
                P.op('dve', lambda e, sm=sm, acc=acc, L=L, jk=jk: e.tensor_scalar(
                    out=jk[:, 0:L], in0=acc[:, 0:L], scalar1=sm[:, 2:3], scalar2=None, op0=ALU.is_ge, op1=ALU.add,
                    accum_out=sm[:, 3:4]), reads=[accB, smB], writes=[jkB, smB])
            for m in mem:
                sm, smB = m['sm'], m['smB']
                P.op('dve', lambda e, sm=sm, i=i: e.scalar_tensor_tensor(out=sm[:, 4:5], in0=sm[:, 3:4], scalar=255.5,
                                                                         in1=sm[:, 8 + i:9 + i], op0=ALU.is_ge,
                                                                         op1=ALU.mult), reads=[smB], writes=[smB])
            for m in mem:
                sm, smB = m['sm'], m['smB']
                P.op('dve', lambda e, sm=sm: e.tensor_tensor(out=sm[:, 1:2], in0=sm[:, 1:2], in1=sm[:, 4:5], op=ALU.add),
                     reads=[smB], writes=[smB])
        return mem

    def indexer_masks(mem):
        for m in mem:
            sm, smB, acc, accB, L, qt = m['sm'], m['smB'], m['acc'], m['accB'], m['L'], m['qt']
            ns, nsB = nsr.next()
            P.op('dve', lambda e, ns=ns, acc=acc, sm=sm, L=L: e.tensor_scalar(out=ns[:, 0:L], in0=acc[:, 0:L],
                                                                              scalar1=sm[:, 1:2], scalar2=None,
                                                                              op0=ALU.is_lt),
                 reads=[accB, smB], writes=[nsB])
            mt, mtB = mtr.next()
            pb = g.PS[7].bitcast(BF16)
            for k0 in range(0, qt + 1, 8):
                k1 = min(qt + 1, k0 + 8)
                for kt in range(k0, k1):
                    P.op('pe', lambda e, kt=kt, k0=k0, ns=ns: e.transpose(out=pb[:, (kt - k0) * 128:(kt - k0 + 1) * 128],
                                                                          in_=ns[:, kt * 128:(kt + 1) * 128],
                                                                          identity=g.ident),
                         reads=[nsB], writes=[g.PSB[7]])
                P.op('act', lambda e, k0=k0, k1=k1, mt=mt: e.copy(out=mt[:, k0 * 128:k1 * 128],
                                                                  in_=pb[:, 0:(k1 - k0) * 128]),
                     reads=[g.PSB[7]], writes=[mtB])
            masks[qt] = (mt, mtB)

    pend_c = [None]

    def attn(qt):
        for gi in range(NKV):
            if qt >= 2:
                mt, mtB = masks[qt]
                bias_fn = lambda kt, mt=mt, mtB=mtB: [(g.negI, mt[:, kt * 128:(kt + 1) * 128], [mtB])]
            else:
                bias_fn = lambda kt, qt=qt: ([(g.negI, g.ctri, [])] if kt == qt else [])
            gqa_attention_tile(g, qt, gi, kT, kB, vS, vB, q4r, g4r, ptr, rlr, tmr, bias_fn, 2, 3)
            if pend_c[0] is not None:
                oproj.chunk(pend_c[0])
                pend_c[0] = None

    for k in range(NT // 2):
        mem = indexer_pair([2 * k + 2, 2 * k + 3]) if k + 1 < NT // 2 else None
        for qt in (2 * k, 2 * k + 1):
            attn(qt)
            if qt % 4 == 3:
                pend_c[0] = qt // 4
                oproj.prefetch(qt // 4)
        if mem is not None:
            indexer_masks(mem)
    if pend_c[0] is not None:
        oproj.chunk(pend_c[0])
    P.barrier(cc=False)
    A.release()
    A.release()


def layer_moba(g, prm):
    P, A = g.P, g.A
    A.mark()
    A.mark()
    cs128, cs128B = load_rope(g, A, g.rope128, 'r128')
    gq, gqB = load_col(g, A, prm['qn'], 128, 1.0, 'gq')
    gk, gkB = load_col(g, A, prm['kn'], 128, float(np.sqrt(128.0)), 'gk')
    blocks = gqa_blocks(gq, gqB, gk, gkB, (g.rm128, cs128, cs128B))
    phase_A2(g, prm['w_in'], blocks)
    A.release()
    A.mark()
    kT = A.alloc(NKV * T, BF16)
    vS = A.alloc(NKV * NT * 128, BF16)
    kB, vB = Buf('kT'), Buf('vS')
    load_kv(g, A, KR0, NKV, 0, NKV, kT, kB, vS, vB)
    km32 = A.alloc(8 * NKV, F32)
    kmb = A.alloc(8 * NKV, BF16)
    kmB = Buf('kmean')
    for gi in range(NKV):
        P.op('dve', lambda e, gi=gi: e.tensor_reduce(out=km32[:, gi * 8:(gi + 1) * 8],
                                                     in_=kT[:, gi * T:(gi + 1) * T].rearrange('p (n s) -> p n s', n=8),
                                                     axis=AX.X, op=ALU.add), reads=[kB], writes=[kmB])
    P.op('dve', lambda e: e.tensor_scalar(out=kmb, in0=km32, scalar1=1.0 / 256.0, scalar2=None, op0=ALU.mult),
         reads=[kmB], writes=[kmB])
    q4r = Ring(A, 'q4', 2, 512, BF16)
    g4r = Ring(A, 'g4', 2, 512, BF16)
    ptr = Ring(A, 'pt', 3, 512, BF16)
    rlr = Ring(A, 'rl', 2, 512, F32)
    tmr = Ring(A, 'tm', 2, 512, F32)
    gsr = Ring(A, 'gs', 2, 16, F32)
    nsr = Ring(A, 'ns', 2, 8, BF16)
    ntr = Ring(A, 'nsT', 2, 128, BF16)
    oproj = OutProj(g, prm['w_out'], NH, [6, 7], prm['hsrc'], prm['oset'], scratch=g.bigT[:, NH * T:KC * T].bitcast(F32))
    cnt = [0]
    pend_c = None
    for qt in range(NT):
        own = qt // 2
        for gi in range(NKV):
            sel = own > 3
            state = {}

            def pre_fn(q4, q4B, gi=gi, own=own, state=state):
                for r in range(4):
                    P.op('pe', lambda e, r=r: e.matmul(g.PS[7][:, 0:8], lhsT=q4[:, r * 128:(r + 1) * 128],
                                                       rhs=kmb[:, gi * 8:(gi + 1) * 8], start=(r == 0), stop=(r == 3)),
                         reads=[q4B, kmB], writes=[g.PSB[7]])
                gs, gsB = gsr.next()
                P.op('dve', lambda e: e.tensor_copy(out=gs[:, 0:8], in_=g.PS[7][:, 0:8]), reads=[g.PSB[7]], writes=[gsB])
                P.op('dve', lambda e: e.memset(gs[:, own:8], -1e30), reads=[gsB], writes=[gsB])
                P.op('dve', lambda e: e.max(out=gs[:, 8:16], in_=gs[:, 0:8]), reads=[gsB], writes=[gsB])
                ns, nsB = nsr.next()
                P.op('dve', lambda e: e.tensor_scalar(out=ns, in0=gs[:, 0:8], scalar1=gs[:, 10:11], scalar2=None,
                                                      op0=ALU.is_lt), reads=[gsB], writes=[nsB])
                pb = g.PS[6].bitcast(BF16)
                P.op('pe', lambda e: e.transpose(out=pb[0:8, 0:128], in_=ns, identity=g.ident), reads=[nsB],
                     writes=[g.PSB[6]])
                nt, ntB = ntr.next()
                P.op('act', lambda e: e.copy(out=nt[0:8, :], in_=pb[0:8, 0:128]), reads=[g.PSB[6]], writes=[ntB])
                state['nt'] = (nt, ntB)

            def bias_fn(kt, qt=qt, own=own, sel=sel, state=state):
                if kt == qt:
                    return [(g.negI, g.ctri, [])]
                n = kt // 2
                if sel and n < own:
                    nt, ntB = state['nt']
                    return [(g.e8[0:8, n * 128:(n + 1) * 128], nt[0:8, :], [ntB])]
                return []

            k = cnt[0] % 2
            cnt[0] += 1
            gqa_attention_tile(g, qt, gi, kT, kB, vS, vB, q4r, g4r, ptr, rlr, tmr, bias_fn, 2 + k, 4 + k,
                               pre_fn=(pre_fn if sel else None))
            if pend_c is not None:
                oproj.chunk(pend_c)
                pend_c = None
        if qt % 4 == 3:
            pend_c = qt // 4
            oproj.prefetch(pend_c)
    if pend_c is not None:
        oproj.chunk(pend_c)
    P.barrier(cc=False)
    A.release()
    A.release()


def layer_ret(g, prm):
    P, A = g.P, g.A
    A.mark()
    A.mark()
    cs, csB = load_rope(g, A, g.rope256, 'r256')
    blocks = []
    QW = RH * 256
    for b in range(QW // 512):
        blocks.append((b * 512, 512, [dict(kind='rope_pair', off=i * 256, row=(b * 2 + i) * 256, cs=(cs, csB), scale=1.0)
                                      for i in range(2)]))
    for b in range(QW // 512):
        blocks.append((QW + b * 512, 512, [dict(kind='rope_pair', off=i * 256, row=QW + (b * 2 + i) * 256,
                                                cs=(cs, csB), scale=1.0 / 16.0) for i in range(2)]))
    for b in range(RH):
        blocks.append((2 * QW + b * 512, 512, [dict(kind='tok', off=0, n=512, vcol=b * 512)]))
    for b in range(RH):
        blocks.append((2 * QW + RH * 512 + b * 512, 512,
                       [dict(kind='gate', off=j * 128, n=128, row=2 * QW + (b * 4 + j) * 128) for j in range(4)]))
    phase_A2(g, prm['w_in'], blocks)
    A.release()
    A.mark()
    scl = A.alloc(RH * 16, F32)
    sclB = Buf('scl')
    P.dma('sp', scl, prm['ret_sc'], writes=[sclB])
    kr = Ring(A, 'kh', 2, 2 * T, BF16)
    vr = Ring(A, 'vh', 2, NT * 512, BF16)
    wr = Ring(A, 'rw', 2, 1024, F32)
    gnr = Ring(A, 'gn', 2, 4, F32)
    q2r = Ring(A, 'q2', 2, 1024, BF16)
    g4r = Ring(A, 'g4', 2, 2048, BF16)
    ptr = Ring(A, 'pt', 3, 512, BF16)
    obr = Ring(A, 'obr', 4, 512, BF16)
    osr = Ring(A, 'osq', 4, 512, BF16)
    mr = Ring(A, 'mean', 2, 512, F32)
    vrr = Ring(A, 'var', 2, 512, F32)
    t1r = Ring(A, 't1', 3, 512, F32)
    for hl in range(RH):
        kh, khB = kr.next()
        vh, vhB = vr.next()
        load_kv(g, A, QW + hl * 256, 2, hl * 512, 4, kh, khB, vh, vhB)
        rw, rwB = wr.next()
        P.dma('sp', rw.rearrange('p (a u) -> p a u', a=2), prm['ret_w'][hl].rearrange('a p u -> p a u'), writes=[rwB])
        gn, gnB = gnr.next()
        P.dma('sp', gn, prm['gn'][0:1, hl * 512:(hl + 1) * 512].rearrange('o (j p) -> p (o j)', p=128), writes=[gnB],
              allow_slow_non_contiguous=True)
        for G4 in range(4):
            q2, q2B = q2r.next()
            g4, g4B = g4r.next()
            P.dma('sp', q2.rearrange('p (c q) -> p c q', c=2),
                  g.zT[hl * 256:(hl + 1) * 256, G4 * 512:(G4 + 1) * 512].rearrange('(c p) q -> p c q', p=128),
                  reads=[g.zTB], writes=[q2B])
            P.dma('sp', g4.rearrange('p (j q) -> p j q', j=4),
                  g.zT[2 * QW + hl * 512:2 * QW + (hl + 1) * 512, G4 * 512:(G4 + 1) * 512].rearrange('(j p) q -> p j q', p=128),
                  reads=[g.zTB], writes=[g4B])
            kts = []
            for kt in range(4 * G4 + 4):
                m = kt - 4 * G4
                c0 = max(0, m) * 128
                if m < 0:
                    dd = 4 * G4 - kt - 1
                    scol = scl[:, hl * 16 + dd: hl * 16 + dd + 1]

                    def ptf(S, PTv, c0, scol=scol, rw=rw, rwB=rwB):
                        return 'dve', (lambda e: e.scalar_tensor_tensor(out=PTv, in0=S, scalar=scol, in1=rw[:, 512:1024],
                                                                        op0=ALU.mult, op1=ALU.mult)), [rwB, sclB]
                else:
                    def ptf(S, PTv, c0, rw=rw, rwB=rwB):
                        return 'dve', (lambda e: e.tensor_tensor(out=PTv, in0=S, in1=rw[:, 0:512 - c0], op=ALU.mult)), [rwB]
                kts.append(dict(k=[kh[:, c * T + kt * 128: c * T + (kt + 1) * 128] for c in range(2)], kB=[khB],
                                v=[vh[:, kt * 512 + j * 128: kt * 512 + (j + 1) * 128] for j in range(4)], vB=[vhB],
                                c0=c0, bias=[], pt=ptf))
            attend(g, 512, [q2[:, 0:512], q2[:, 512:1024]], [q2B], kts, 4, [0, 1], [2, 3, 4, 5], None, ptr, onesL=False)
            flush_att(g)
            obs = []
            for j in range(4):
                ob, obB = obr.next()
                os_, osB = osr.next()
                P.op('act', lambda e, ob=ob, j=j: e.copy(out=ob, in_=g.PS[2 + j]), reads=[g.PSB[2 + j]], writes=[obB])
                P.op('act', lambda e, os_=os_, j=j: e.activation(out=os_, in_=g.PS[2 + j], func=AF.Square),
                     reads=[g.PSB[2 + j]], writes=[osB])
                obs.append((ob, obB, os_, osB))
            for j in range(4):
                P.op('pe', lambda e, j=j, obs=obs: e.matmul(g.PS[6], lhsT=g.ones, rhs=obs[j][0], start=(j == 0),
                                                            stop=(j == 3)), reads=[obs[j][1]], writes=[g.PSB[6]])
            for j in range(4):
                P.op('pe', lambda e, j=j, obs=obs: e.matmul(g.PS[7], lhsT=g.ones, rhs=obs[j][2], start=(j == 0),
                                                            stop=(j == 3)), reads=[obs[j][3]], writes=[g.PSB[7]])
            mean, meanB = mr.next()
            var, varB = vrr.next()
            P.op('act', lambda e, mean=mean: e.activation(out=mean, in_=g.PS[6], func=AF.Copy, scale=1.0 / 512.0),
                 reads=[g.PSB[6]], writes=[meanB])
            P.op('pool', lambda e, var=var, mean=mean: e.tensor_tensor(out=var, in0=mean, in1=mean, op=ALU.mult),
                 reads=[meanB], writes=[varB])
            P.op('dve', lambda e, var=var: e.scalar_tensor_tensor(out=var, in0=g.PS[7], scalar=1.0 / 512.0, in1=var,
                                                                  op0=ALU.mult, op1=ALU.subtract),
                 reads=[g.PSB[7], varB], writes=[varB])
            P.op('act', lambda e, var=var: e.activation(out=var, in_=var, func=AF.Ln, bias=g.c_eps[:, 2:3]),
                 reads=[varB], writes=[varB])
            P.op('act', lambda e, var=var: e.activation(out=var, in_=var, func=AF.Exp, scale=-0.5), reads=[varB],
                 writes=[varB])
            for j in range(4):
                t1, t1B = t1r.next()
                P.op('dve', lambda e, t1=t1, j=j, mean=mean: e.tensor_tensor(out=t1, in0=g.PS[2 + j], in1=mean,
                                                                            op=ALU.subtract),
                     reads=[g.PSB[2 + j], meanB], writes=[t1B])
                P.op('dve', lambda e, t1=t1, var=var, gn=gn, j=j: e.scalar_tensor_tensor(
                    out=t1, in0=t1, scalar=gn[:, j:j + 1], in1=var, op0=ALU.mult, op1=ALU.mult),
                     reads=[t1B, varB, gnB], writes=[t1B])
                dst = g.bigT[:, (hl * 4 + j) * T + G4 * 512:(hl * 4 + j) * T + (G4 + 1) * 512]
                P.op('pool', lambda e, t1=t1, g4=g4, j=j, dst=dst: e.tensor_tensor(
                    out=dst, in0=t1, in1=g4[:, j * 512:(j + 1) * 512], op=ALU.mult),
                     reads=[t1B, g4B], writes=[g.bigB])
    P.barrier()
    A.release()
    A.mark()
    oproj = OutProj(g, prm['w_out'], RH * 4, [0, 1, 2, 3], prm['hsrc'], prm['oset'], nbuf=4)
    for j in range(4):
        oproj.chunk(j)
    P.barrier(cc=False)
    A.release()
    A.release()


def layer_fox(g, prm):
    P, A = g.P, g.A
    HW = NH * 128
    A.mark()
    c3 = A.alloc(T, BF16)
    cT = A.alloc(NT * 16, F32)
    A.mark()
    cpos = A.alloc(T, F32)
    spl = A.alloc(T, F32)
    hi16 = A.alloc(T, BF16)
    mid16 = A.alloc(T, BF16)
    lo16 = A.alloc(T, BF16)
    onesr = A.alloc(T, F32)
    cB = Buf('cfox')
    c3B = Buf('c3')
    fb = A.alloc(1, F32)
    fbB = Buf('fb')
    P.op('pool', lambda e: e.memset(fb[0:80, :], 0.0), writes=[fbB])
    for o in (0, 32, 64):
        P.dma('sp', fb[o:o + NH, :], prm['fb'].rearrange('o d -> d o'), reads=[fbB], writes=[fbB])
    P.op('pool', lambda e: e.tensor_scalar(out=fb[0:80, :], in0=fb[0:80, :], scalar1=-1.0, scalar2=None, op0=ALU.mult),
         reads=[fbB], writes=[fbB])
    P.op('pool', lambda e: e.memset(onesr[0:80, :], 1.0), writes=[cB])
    P.op('pool', lambda e: e.memset(c3[0:80, :], 0.0), writes=[c3B])
    A.mark()
    gq, gqB = load_col(g, A, prm['qn'], 128, 1.0, 'gq')
    gk, gkB = load_col(g, A, prm['kn'], 128, float(np.sqrt(128.0)), 'gk')
    blocks = []
    for b in range(HW // 512):
        blocks.append((b * 512, 512, [dict(kind='qk', off=j * 128, n=128, row=(b * 4 + j) * 128, gcol=gq, gB=gqB, rope=None)
                                      for j in range(4)]))
    for b in range(HW // 512):
        blocks.append((HW + b * 512, 512, [dict(kind='qk', off=j * 128, n=128, row=HW + (b * 4 + j) * 128, gcol=gk,
                                                gB=gkB, rope=None) for j in range(4)]))
    for b in range(HW // 512):
        blocks.append((2 * HW + b * 512, 512, [dict(kind='tok', off=0, n=512, vcol=b * 512)]))
    for b in range(HW // 512):
        blocks.append((3 * HW + b * 512, 512, [dict(kind='gate', off=j * 128, n=128, row=2 * HW + (b * 4 + j) * 128)
                                               for j in range(4)]))

    def f_loader(wv, wB):
        P.op('pool', lambda e: e.memset(wv[:, :, 0:80], 0.0), writes=[wB])
        for o in (0, 32, 64):
            P.dma('pool', wv[:, :, o:o + NH], prm['w_in'][:, 4 * HW:4 * HW + NH].rearrange('(k p) n -> p k n', p=128),
                  reads=[wB], writes=[wB])

    blocks.append((4 * HW, NH, [dict(kind='fox_f', off=0, n=80, sb_dst=(spl, cB), fbias=fb, fbB=fbB)], f_loader))
    phase_A2(g, prm['w_in'], blocks)
    A.release()
    P.op('dve', lambda e: e.tensor_tensor_scan(out=cpos[0:80, :], data0=onesr[0:80, :], data1=spl[0:80, :], initial=0.0,
                                               op0=ALU.mult, op1=ALU.add), reads=[cB], writes=[cB])
    P.op('dve', lambda e: e.tensor_copy(out=hi16[0:80, :], in_=cpos[0:80, :]), reads=[cB], writes=[cB])
    P.op('dve', lambda e: e.tensor_tensor(out=spl[0:80, :], in0=cpos[0:80, :], in1=hi16[0:80, :], op=ALU.subtract),
         reads=[cB], writes=[cB])
    P.op('dve', lambda e: e.tensor_copy(out=mid16[0:80, :], in_=spl[0:80, :]), reads=[cB], writes=[cB])
    P.op('dve', lambda e: e.tensor_tensor(out=spl[0:80, :], in0=spl[0:80, :], in1=mid16[0:80, :], op=ALU.subtract),
         reads=[cB], writes=[cB])
    P.op('dve', lambda e: e.tensor_copy(out=lo16[0:80, :], in_=spl[0:80, :]), reads=[cB], writes=[cB])
    P.op('pool', lambda e: e.tensor_copy(out=c3[0:16, :], in_=hi16[0:16, :]), reads=[cB, c3B], writes=[c3B])
    P.op('pool', lambda e: e.tensor_copy(out=c3[32:48, :], in_=mid16[32:48, :]), reads=[cB, c3B], writes=[c3B])
    P.op('pool', lambda e: e.tensor_copy(out=c3[64:80, :], in_=lo16[64:80, :]), reads=[cB, c3B], writes=[c3B])
    for kt in range(NT):
        P.op('pe', lambda e, kt=kt: e.transpose(out=g.PS[7][:, kt * 16:(kt + 1) * 16], in_=cpos[0:16, kt * 128:(kt + 1) * 128],
                                                identity=g.identf[0:16, 0:16]), reads=[cB], writes=[g.PSB[7]])
    cTB = Buf('cT')
    P.op('act', lambda e: e.copy(out=cT, in_=g.PS[7][:, 0:NT * 16]), reads=[g.PSB[7]], writes=[cTB])
    P.barrier()
    A.release()
    A.mark()
    kr = Ring(A, 'kh', 2, T, BF16)
    vr = Ring(A, 'vh', 2, NT * 128, BF16)
    q4r = Ring(A, 'q4', 2, 512, BF16)
    g4r = Ring(A, 'g4', 2, 512, BF16)
    ptr = Ring(A, 'pt', 3, 512, BF16)
    rlr = Ring(A, 'rl', 2, 512, F32)
    tmr = Ring(A, 'tm', 2, 512, F32)
    oproj = OutProj(g, prm['w_out'], NH, [6, 7], prm['hsrc'], prm['oset'], scratch=g.bigT[:, NH * T:KC * T].bitcast(F32))
    cnt = 0
    pend_c = None
    for G4 in range(4):
        nkt = 4 * G4 + 4
        for h in range(NH):
            kh, khB = kr.next()
            vh, vhB = vr.next()
            load_kv(g, A, HW + h * 128, 1, h * 128, 1, kh, khB, vh, vhB, nkt=nkt)
            q4, q4B = q4r.next()
            g4, g4B = g4r.next()
            P.dma('sp', q4, g.zT[h * 128:(h + 1) * 128, G4 * 512:(G4 + 1) * 512], reads=[g.zTB], writes=[q4B])
            P.dma('sp', g4, g.zT[2 * HW + h * 128:2 * HW + (h + 1) * 128, G4 * 512:(G4 + 1) * 512], reads=[g.zTB],
                  writes=[g4B])
            kts = []
            for kt in range(nkt):
                m = kt - 4 * G4
                c0 = max(0, m) * 128
                bias = [(g.e3[0:80, h * 128:(h + 1) * 128], c3[0:80, G4 * 512 + c0:(G4 + 1) * 512], c0, 512, [c3B])]
                if m >= 0:
                    bias.append((g.negI, g.ctri, c0, c0 + 128, []))
                kts.append(dict(k=[kh[:, kt * 128:(kt + 1) * 128]], kB=[khB], v=[vh[:, kt * 128:(kt + 1) * 128]], vB=[vhB],
                                c0=c0, bias=bias, pt=pt_exp(cT[:, kt * 16 + h:kt * 16 + h + 1], [cTB])))
            k = cnt % 2
            cnt += 1
            dst = g.bigT[:, h * T + G4 * 512:h * T + (G4 + 1) * 512]
            attend(g, 512, [q4], [q4B], kts, 1, [0, 1], [2 + k], 4 + k, ptr,
                   finish=(lambda k=k, g4=g4, g4B=g4B, dst=dst: softmax_finish_act(g, 512, 2 + k, 4 + k, g4, [g4B], dst,
                                                                             rlr, tmr)))
            if pend_c is not None:
                oproj.chunk(pend_c)
                pend_c = None
        pend_c = G4
        oproj.prefetch(G4)
    oproj.chunk(pend_c)
    P.barrier(cc=False)
    A.release()
    A.release()


W_SHAPES = {
    'a_norm': [1, D], 'a_w_in': [D, 2 * NH * 128 + 2 * NKV * 128 + 1104], 'a_q_norm': [1, 128], 'a_k_norm': [1, 128],
    'a_w_out': [NH * 128, D],
    'b_norm': [1, D], 'b_w_in': [D, 2 * NH * 128 + 2 * NKV * 128], 'b_q_norm': [1, 128], 'b_k_norm': [1, 128],
    'b_w_out': [NH * 128, D],
    'c_norm': [1, D], 'c_w_in': [D, RH * 1536], 'c_gn': [1, RH * 512], 'c_w_out': [RH * 512, D],
    'd_norm': [1, D], 'd_w_in': [D, 4 * NH * 128 + NH], 'd_f_bias': [1, NH], 'd_q_norm': [1, 128], 'd_k_norm': [1, 128],
    'd_w_out': [NH * 128, D],
}
C_SHAPES = {'c_mats': [128, 768], 'c_f32': [128, 264], 'c_e8': [8, 1024], 'c_e3': [80, 2048],
            'rope128': [2, 128, T], 'rope64': [2, 128, T], 'rope256': [2, 128, T], 'ret_w': [RH, 2, 128, 512],
            'ret_sc': [128, RH * 16]}


def build(layers=(0, 1, 2, 3), debug=False):
    nc = bass.Bass("TRN2", target_bir_lowering=False)
    x = nc.dram_tensor("x", [T, D], F32, kind="ExternalInput").ap()
    out = nc.dram_tensor("out", [T, D], F32, kind="ExternalOutput").ap()
    w = {k: nc.dram_tensor(k, s, F32, kind="ExternalInput").ap() for k, s in W_SHAPES.items()}
    cst = {k: nc.dram_tensor(k, s, F32, kind="ExternalInput").ap() for k, s in C_SHAPES.items()}
    dk = dict(kind="ExternalOutput") if debug else {}
    zT = nc.dram_tensor("zT", [4096, T], BF16, **dk).ap()
    vtok = nc.dram_tensor("vtok", [T, 2048], BF16, **dk).ap()
    ctx = ExitStack()
    with ctx:
        g = G()
        g.nc = nc
        g.P = P = Prog(nc, ctx)
        P.flush_hook = lambda: flush_att(g)
        art = ctx.enter_context(nc.sbuf_tensor("arena", [128, ARENA_WORDS], F32))
        g.A = A = Arena(art, ARENA_WORDS)
        pst = [ctx.enter_context(nc.psum_tensor("ps%d" % i, [128, 512], F32)) for i in range(8)]
        g.PS = [t.ap() for t in pst]
        g.PSB = [Buf('ps%d' % i) for i in range(8)]
        g.zT, g.vtok, g.zTB, g.vtB = zT, vtok, Buf('zT'), Buf('vtok')
        g.ypart_t = [nc.dram_tensor("ypart%d" % j, [512, D], F32) for j in range(4)]
        g.ysum_t = [[nc.dram_tensor("ysum%d_%d" % (k, j), [512, D], F32) for j in range(4)] for k in range(2)]
        g.ypart = [t.ap() for t in g.ypart_t]
        g.ysum = [[t.ap() for t in ts] for ts in g.ysum_t]
        g.ypB = [Buf('yp%d' % j) for j in range(4)]
        g.ysB = [[Buf('ys%d_%d' % (k, j)) for j in range(4)] for k in range(2)]
        g.rope128, g.rope64, g.rope256 = cst['rope128'], cst['rope64'], cst['rope256']
        g.bigT = A.alloc(KC * T, BF16)
        g.bigB = Buf('bigT')
        mats = A.alloc(768, BF16)
        cB = Buf('consts')
        P.dma('pool', mats, cst['c_mats'], writes=[cB])
        g.ident, g.negI, g.ones = mats[:, 0:128], mats[:, 128:256], mats[:, 256:384]
        g.rm128, g.rm64, g.ctri = mats[:, 384:512], mats[:, 512:640], mats[:, 640:768]
        cf = A.alloc(264, F32)
        P.dma('sp', cf, cst['c_f32'], writes=[cB])
        g.negtri = cf[:, 0:128]
        g.c_eps = cf[:, 128:131]
        g.identf = cf[:, 136:264]
        g.c_one = A.alloc(1, F32)
        P.op('pool', lambda e: e.memset(g.c_one, 1.0), writes=[cB])
        g.e8 = A.alloc(1024, BF16)
        P.dma('pool', g.e8[0:8, :], cst['c_e8'], writes=[cB])
        g.e3 = A.alloc(2048, BF16)
        P.dma('pool', g.e3[0:80, :], cst['c_e3'], writes=[cB])
        P.barrier()

        hB_x = [Buf('hx%d' % i) for i in range(NT)]
        hB_o = [Buf('ho%d' % i) for i in range(4)]
        norms = {0: 'a_norm', 1: 'b_norm', 2: 'c_norm', 3: 'd_norm'}
        for li, L in enumerate(layers):
            if li == 0:
                hsrc = lambda tt: (x[tt * 128:(tt + 1) * 128, :], hB_x[tt])
            else:
                hsrc = lambda tt, k=(li - 1) % 2: (g.ysum[k][tt // 4][(tt % 4) * 128:(tt % 4 + 1) * 128, :],
                                                   g.ysB[k][tt // 4])
            oset = li % 2
            phase_A1(g, hsrc, w[norms[L]])
            if L == 0:
                layer_dsa(g, dict(w_in=w['a_w_in'], qn=w['a_q_norm'], kn=w['a_k_norm'], w_out=w['a_w_out'], hsrc=hsrc,
                                  oset=oset))
            elif L == 1:
                layer_moba(g, dict(w_in=w['b_w_in'], qn=w['b_q_norm'], kn=w['b_k_norm'], w_out=w['b_w_out'], hsrc=hsrc,
                                   oset=oset))
            elif L == 2:
                layer_ret(g, dict(w_in=w['c_w_in'], gn=w['c_gn'], w_out=w['c_w_out'], ret_w=cst['ret_w'],
                                  ret_sc=cst['ret_sc'], hsrc=hsrc, oset=oset))
            else:
                layer_fox(g, dict(w_in=w['d_w_in'], fb=w['d_f_bias'], qn=w['d_q_norm'], kn=w['d_k_norm'],
                                  w_out=w['d_w_out'], hsrc=hsrc, oset=oset))
        final_copy(g, (len(layers) - 1) % 2, out, hB_o)
        P.barrier()
        P.emit()
        g.stats = dict(peak_words=A.peak, ops={k: len(v) for k, v in P.q.items()})
    return nc, g.stats


def shard_weights(weights, s):
    def cat(a, pieces):
        return np.ascontiguousarray(np.concatenate([a[..., lo:hi] for lo, hi in pieces], axis=-1))
    hq, hk = NH * 128, NKV * 128
    o = {}
    A_ = weights['a_w_in'][0]
    o['a_w_in'] = cat(A_, [(s * hq, (s + 1) * hq), (2048 + s * hk, 2048 + (s + 1) * hk),
                           (2560 + s * hk, 2560 + (s + 1) * hk), (3072 + s * hq, 3072 + (s + 1) * hq), (5120, 6224)])
    o['a_w_out'] = np.ascontiguousarray(weights['a_w_out'][0][s * hq:(s + 1) * hq])
    B_ = weights['b_w_in'][0]
    o['b_w_in'] = cat(B_, [(s * hq, (s + 1) * hq), (2048 + s * hk, 2048 + (s + 1) * hk),
                           (2560 + s * hk, 2560 + (s + 1) * hk), (3072 + s * hq, 3072 + (s + 1) * hq)])
    o['b_w_out'] = np.ascontiguousarray(weights['b_w_out'][0][s * hq:(s + 1) * hq])
    C_ = weights['c_w_in'][0]
    rq, rv = RH * 256, RH * 512
    o['c_w_in'] = cat(C_, [(s * rq, (s + 1) * rq), (2048 + s * rq, 2048 + (s + 1) * rq),
                           (4096 + s * rv, 4096 + (s + 1) * rv), (8192 + s * rv, 8192 + (s + 1) * rv)])
    o['c_gn'] = np.ascontiguousarray(weights['c_gn'][0][s * rv:(s + 1) * rv]).reshape(1, rv)
    o['c_w_out'] = np.ascontiguousarray(weights['c_w_out'][0][s * rv:(s + 1) * rv])
    D_ = weights['d_w_in'][0]
    o['d_w_in'] = cat(D_, [(s * hq, (s + 1) * hq), (2048 + s * hq, 2048 + (s + 1) * hq),
                           (4096 + s * hq, 4096 + (s + 1) * hq), (6144 + s * hq, 6144 + (s + 1) * hq),
                           (8192 + s * NH, 8192 + (s + 1) * NH)])
    o['d_f_bias'] = np.ascontiguousarray(weights['d_f_bias'][0][s * NH:(s + 1) * NH]).reshape(1, NH)
    o['d_w_out'] = np.ascontiguousarray(weights['d_w_out'][0][s * hq:(s + 1) * hq])
    for k in ('a_norm', 'a_q_norm', 'a_k_norm', 'b_norm', 'b_q_norm', 'b_k_norm', 'c_norm', 'd_norm', 'd_q_norm',
              'd_k_norm'):
        o[k] = np.ascontiguousarray(weights[k][0]).reshape(1, -1)
    return o


def shard_consts(cst, s):
    o = {k: cst[k] for k in C_SHAPES if k in cst}
    o['ret_w'] = np.ascontiguousarray(cst['ret_w'][s * RH:(s + 1) * RH])
    lg = cst['ret_lg'][s * RH:(s + 1) * RH]
    sc = np.exp(lg[:, None] * 128.0 * np.arange(16)[None, :]).astype(np.float32).reshape(1, RH * 16)
    o['ret_sc'] = np.ascontiguousarray(np.broadcast_to(sc, (128, RH * 16)))
    return o


_CACHE = {}
LAST = None


def run_layers(x4, weights, layers=(0, 1, 2, 3), n_cores=8, debug=False):
    global LAST
    key = (tuple(layers), debug)
    if key not in _CACHE:
        _CACHE[key] = build(layers, debug)
    nc, stats = _CACHE[key]
    cst = host_consts()
    B = x4.shape[0]
    wsh = [shard_weights(weights, s) for s in range(TP)]
    csh = [shard_consts(cst, s) for s in range(TP)]
    in_maps = []
    for c in range(n_cores):
        b, s = (c // TP) % B, c % TP
        m = {'x': np.ascontiguousarray(x4[b], dtype=np.float32)}
        for k in W_SHAPES:
            m[k] = np.ascontiguousarray(wsh[s][k], dtype=np.float32).reshape(W_SHAPES[k])
        for k in C_SHAPES:
            m[k] = np.ascontiguousarray(csh[s][k], dtype=np.float32).reshape(C_SHAPES[k])
        in_maps.append(m)
    res = run_bass_kernel_spmd(nc, in_maps, core_ids=list(range(n_cores)))
    LAST = res
    return np.stack([res.results[TP * b]['out'] for b in range(B)])


def kernel(**inputs):
    x = np.asarray(inputs['x'], dtype=np.float32)
    weights = {k: np.asarray(inputs[k], dtype=np.float32) for k in
               ('a_norm', 'a_w_in', 'a_q_norm', 'a_k_norm', 'a_w_out', 'b_norm', 'b_w_in', 'b_q_norm', 'b_k_norm',
                'b_w_out', 'c_norm', 'c_w_in', 'c_gn', 'c_w_out', 'd_norm', 'd_w_in', 'd_f_bias', 'd_q_norm',
                'd_k_norm', 'd_w_out')}
    return run_layers(x, weights).astype(np.float32)
```

```python
import numpy as np
from contextlib import ExitStack
import concourse.bass as bass
import concourse.mybir as mybir
from concourse.bass_utils import run_bass_kernel_spmd

F32 = mybir.dt.float32
BF16 = mybir.dt.bfloat16
AF = mybir.ActivationFunctionType
ALU = mybir.AluOpType
AX = mybir.AxisListType

T = 2048
D = 2048
NT = 16
KC = 16
EPS = 1e-6
NEGB = -30000.0
ENGS = ('sp', 'act', 'dve', 'pool', 'pe')
NDS = 16
ARENA_WORDS = 52800


class Buf:
    __slots__ = ('name', 'w', 'r')

    def __init__(self, name):
        self.name = name
        self.w = None
        self.r = []


class Prog:
    def __init__(self, nc, ctx):
        self.nc = nc
        self.q = {k: [] for k in ENGS}
        self.esem = {k: ctx.enter_context(nc.semaphore('es_' + k)) for k in ('act', 'dve', 'pool', 'pe')}
        self.ecnt = {k: 0 for k in self.esem}
        self.seen = {k: {} for k in ENGS}
        self.dsem = {k: [ctx.enter_context(nc.semaphore('ds_%s%d' % (k, i))) for i in range(NDS)]
                     for k in ('sp', 'pool', 'act')}
        self.dcnt = {k: 0 for k in self.dsem}
        self.ccsem = ctx.enter_context(nc.semaphore('cc_sem'))
        self.ccnt = 0

    def _collect(self, eng, reads, writes):
        deps = []
        for b in reads:
            if b.w is not None:
                deps.append(b.w)
        for b in writes:
            if b.w is not None:
                deps.append(b.w)
            for ev in b.r:
                if ev[2] is None or ev[2] != eng or eng == 'pool':
                    deps.append(ev)
        return deps

    def _filter(self, eng, deps):
        need = {}
        for sem, val, peng in deps:
            if peng == 'pe' and eng == 'pe':
                continue
            k = id(sem)
            if self.seen[eng].get(k, 0) >= val:
                continue
            if k not in need or need[k][1] < val:
                need[k] = (sem, val)
        out = []
        for k, (sem, val) in need.items():
            self.seen[eng][k] = val
            out.append((sem, val))
        return out

    def _commit(self, ev, reads, writes):
        for b in reads:
            b.r.append(ev)
        for b in writes:
            b.w = ev
            b.r = []

    def op(self, eng, fn, reads=(), writes=()):
        deps = self._collect(eng, reads, writes)
        waits = self._filter(eng, deps)
        self.ecnt[eng] += 1
        sem = self.esem[eng]
        ev = (sem, self.ecnt[eng], eng)
        self.q[eng].append((waits, fn, (sem, 1)))
        self._commit(ev, reads, writes)
        return ev

    def dma(self, qeng, out, in_, reads=(), writes=(), **kw):
        i = self.dcnt[qeng]
        self.dcnt[qeng] += 1
        slot, gen = i % NDS, i // NDS
        sem = self.dsem[qeng][slot]
        deps = self._collect(None, reads, writes)
        if gen > 0:
            deps.append((sem, 16 * gen, None))
        waits = self._filter(qeng, deps)
        ev = (sem, 16 * (gen + 1), None)
        self.q[qeng].append((waits, (lambda e, out=out, in_=in_, kw=kw: e.dma_start(out=out, in_=in_, **kw)),
                             (sem, 16)))
        self._commit(ev, reads, writes)
        return ev

    def coll(self, fn, reads=(), writes=()):
        deps = self._collect(None, reads, writes)
        waits = self._filter('pool', deps)
        self.ccnt += 1
        ev = (self.ccsem, self.ccnt, None)
        self.q['pool'].append((waits, fn, (self.ccsem, 1)))
        self._commit(ev, reads, writes)
        return ev

    def barrier(self, cc=True):
        if getattr(self, 'flush_hook', None) is not None:
            self.flush_hook()
        evs = []
        if cc and self.ccnt > 0:
            evs.append((self.ccsem, self.ccnt, None))
        for k in self.esem:
            if self.ecnt[k] > 0:
                evs.append((self.esem[k], self.ecnt[k], k))
        for qn in self.dsem:
            n = self.dcnt[qn]
            for slot in range(NDS):
                cnt = (n - slot + NDS - 1) // NDS if n > slot else 0
                if cnt > 0:
                    evs.append((self.dsem[qn][slot], 16 * cnt, None))
        for eng in ENGS:
            mine = [ev for ev in evs if ev[2] is None or ev[2] != eng]
            waits = self._filter(eng, mine)
            if waits:
                self.q[eng].append((waits, None, None))

    def emit(self):
        nc = self.nc
        qs = self.q

        def run(e, ops):
            for waits, fn, inc in ops:
                for sem, val in waits:
                    e.wait_ge(sem, val)
                if fn is not None:
                    ins = fn(e)
                    if inc is not None:
                        ins.then_inc(inc[0], inc[1])

        with nc.Block() as block:
            @block.sync
            def _(e):
                run(e, qs['sp'])

            @block.scalar
            def _(e):
                run(e, qs['act'])

            @block.vector
            def _(e):
                run(e, qs['dve'])

            @block.gpsimd
            def _(e):
                run(e, qs['pool'])

            @block.tensor
            def _(e):
                run(e, qs['pe'])


class Arena:
    def __init__(self, t, nwords):
        self.t = t
        self.nwords = nwords
        self.off = 0
        self.marks = []
        self.peak = 0

    def mark(self):
        self.marks.append(self.off)

    def release(self):
        self.off = self.marks.pop()

    def alloc(self, n, dtype, parts=128):
        words = n if dtype == F32 else (n + 1) // 2
        assert self.off + words <= self.nwords, ('arena overflow', self.off, words, self.nwords)
        ap = self.t[0:parts, self.off:self.off + words]
        self.off += words
        self.peak = max(self.peak, self.off)
        if dtype != F32:
            ap = ap.bitcast(dtype)[:, 0:n]
        return ap


class Ring:
    def __init__(self, A, name, n, size, dtype, parts=128):
        self.aps = [A.alloc(size, dtype, parts) for _ in range(n)]
        self.bufs = [Buf('%s%d' % (name, i)) for i in range(n)]
        self.i = 0

    def next(self):
        k = self.i % len(self.aps)
        self.i += 1
        return self.aps[k], self.bufs[k]


class G:
    pass


def _rot_mat(hd):
    R = np.zeros((128, 128), np.float32)
    half = hd // 2
    for m in range(128):
        b, r = divmod(m, hd)
        if r < half:
            R[b * hd + r + half, m] = -1.0
        else:
            R[b * hd + r - half, m] = 1.0
    return R


def _rope_tab(hd):
    i = (np.arange(128) % (hd // 2)).astype(np.float32)
    inv = (np.float32(10000.0) ** (-(2.0 * i).astype(np.float32) / np.float32(hd))).astype(np.float32)
    pos = np.arange(T, dtype=np.float32)
    ang = (pos[None, :] * inv[:, None]).astype(np.float32)
    return np.stack([np.cos(ang), np.sin(ang)]).astype(np.float32)


def host_consts():
    c = {}
    I = np.eye(128, dtype=np.float32)
    ctri = (np.arange(128)[:, None] > np.arange(128)[None, :]).astype(np.float32)
    c['c_mats'] = np.concatenate([I, NEGB * I, np.ones((128, 128), np.float32), _rot_mat(128), _rot_mat(64), ctri],
                                 axis=1)
    negtri = np.where(np.arange(128)[None, :] > np.arange(128)[:, None], -1e30, 0.0).astype(np.float32)
    misc = np.zeros((128, 8), np.float32)
    misc[:, 0] = D * EPS
    misc[:, 1] = 128 * EPS
    misc[:, 2] = EPS
    c['c_f32'] = np.concatenate([negtri, misc, I], axis=1)
    e8 = np.zeros((8, 8, 128), np.float32)
    for n in range(8):
        e8[n, n, :] = NEGB
    c['c_e8'] = e8.reshape(8, 1024)
    e16 = np.zeros((16, 16, 128), np.float32)
    for n in range(16):
        e16[n, n, :] = 1.0
    c['c_e16'] = e16.reshape(16, 2048)
    e3 = np.zeros((80, 16, 128), np.float32)
    for h in range(16):
        for o in (0, 32, 64):
            e3[o + h, h, :] = -1.0
    c['c_e3'] = e3.reshape(80, 2048)
    c['rope128'] = _rope_tab(128)
    c['rope64'] = _rope_tab(64)
    c['rope256'] = _rope_tab(256)
    lg = np.log(1.0 - 2.0 ** (-5.0 - np.arange(8, dtype=np.float64)))
    s = np.arange(128, dtype=np.float64)[:, None]
    u = np.arange(512, dtype=np.float64)[None, :]
    rw = np.zeros((8, 2, 128, 512), np.float32)
    for h in range(8):
        rw[h, 0] = np.where(u >= s, np.exp(lg[h] * np.maximum(u - s, 0.0)), 0.0)
        rw[h, 1] = np.exp(lg[h] * (u - s + 128.0))
    c['ret_w'] = rw
    c['ret_lg'] = lg
    return c


RET_LG = np.log(1.0 - 2.0 ** (-5.0 - np.arange(8, dtype=np.float64)))


def phase_A1(g, hsrc, norm_ap):
    P, A = g.P, g.A
    A.mark()
    gb = A.alloc(D, F32)
    gbB = Buf('gb')
    P.dma('sp', gb, norm_ap.partition_broadcast(128), writes=[gbB])
    P.op('pool', lambda e: e.tensor_scalar(out=gb, in0=gb, scalar1=float(np.sqrt(float(D))), scalar2=0.0,
                                           op0=ALU.mult, op1=ALU.add), reads=[gbB], writes=[gbB])
    xr = Ring(A, 'xt', 3, D, F32)
    jr = Ring(A, 'jk', 1, D, BF16)
    hr = Ring(A, 'hn', 3, D, BF16)
    sr = Ring(A, 'ssq', 3, 2, F32)
    def stage0(tt):
        xt, xb = xr.next()
        jk, jb = jr.next()
        hn, hb = hr.next()
        ss, sb = sr.next()
        hap, hB_ = hsrc(tt)
        P.dma('sp', xt, hap, reads=[hB_], writes=[xb])
        P.op('act', lambda e: e.activation(out=jk, in_=xt, func=AF.Square, accum_out=ss[:, 0:1]),
             reads=[xb], writes=[jb, sb])
        P.op('act', lambda e: e.activation(out=ss[:, 1:2], in_=ss[:, 0:1], func=AF.Ln, bias=g.c_eps[:, 0:1]),
             reads=[sb], writes=[sb])
        P.op('act', lambda e: e.activation(out=ss[:, 1:2], in_=ss[:, 1:2], func=AF.Exp, scale=-0.5),
             reads=[sb], writes=[sb])
        P.op('dve', lambda e: e.scalar_tensor_tensor(out=hn, in0=xt, scalar=ss[:, 1:2], in1=gb, op0=ALU.mult,
                                                     op1=ALU.mult), reads=[xb, sb, gbB], writes=[hb])
        return hn, hb

    def stage1(tt, hn, hb):
        for half in range(2):
            bk = (tt % 2) * 2 + half
            pb = g.PS[bk].bitcast(BF16)
            for j in range(8):
                kc = half * 8 + j
                P.op('pe', lambda e, pb=pb, j=j, kc=kc: e.transpose(out=pb[:, j * 128:(j + 1) * 128],
                                                                   in_=hn[:, kc * 128:(kc + 1) * 128],
                                                                   identity=g.ident),
                     reads=[hb], writes=[g.PSB[bk]])
            dst = g.bigT[:, half * 8 * T:(half * 8 + 8) * T].rearrange('p (a t) -> p a t', a=8)[:, :, tt * 128:(tt + 1) * 128]
            src = pb.rearrange('p (a b) -> p a b', a=8)
            if half == 0:
                P.op('act', lambda e, dst=dst, src=src: e.copy(out=dst, in_=src), reads=[g.PSB[bk]], writes=[g.bigB])
            else:
                P.op('dve', lambda e, dst=dst, src=src: e.tensor_copy(out=dst, in_=src), reads=[g.PSB[bk]],
                     writes=[g.bigB])

    prev = None
    for tt in range(NT + 1):
        cur = stage0(tt) if tt < NT else None
        if prev is not None:
            stage1(tt - 1, prev[0], prev[1])
        prev = cur
    P.barrier()
    A.release()


def phase_A2(g, w_in, blocks):
    P, A = g.P, g.A
    A.mark()
    wr = Ring(A, 'wbf', 2, KC * 512, BF16)
    sqr = Ring(A, 'sqb', 2, 512, BF16)
    rsr = Ring(A, 'rstd', 2, 512, F32)
    qnr = Ring(A, 'qn', 2, 512, BF16)
    t1r = Ring(A, 't1', 2, 512, F32)
    t2r = Ring(A, 't2', 2, 512, F32)
    obr = Ring(A, 'ob', 4, 512, BF16)
    zi = [0]
    si = [0]
    ri = [0]

    def zbank():
        b = zi[0] % 4
        zi[0] += 1
        return b

    def proj_feat(wv, wB, loff, M, tg, bank):
        for kc in range(KC):
            P.op('pe', lambda e, kc=kc: e.matmul(g.PS[bank][0:M, :], lhsT=wv[:, kc, loff:loff + M],
                                                 rhs=g.bigT[:, kc * T + tg * 512: kc * T + (tg + 1) * 512],
                                                 start=(kc == 0), stop=(kc == KC - 1)),
                 reads=[wB, g.bigB], writes=[g.PSB[bank]])

    def rmsn(bank, M, gcol, gB):
        sq, sqB = sqr.next()
        P.op('act', lambda e: e.activation(out=sq[0:M, :], in_=g.PS[bank][0:M, :], func=AF.Square),
             reads=[g.PSB[bank]], writes=[sqB])
        sb = 4 + (si[0] % 2)
        si[0] += 1
        P.op('pe', lambda e: e.matmul(g.PS[sb][0:M, :], lhsT=g.ones[0:M, 0:M], rhs=sq[0:M, :], start=True, stop=True),
             reads=[sqB], writes=[g.PSB[sb]])
        rs, rsB = rsr.next()
        P.op('act', lambda e: e.activation(out=rs[0:M, :], in_=g.PS[sb][0:M, :], func=AF.Ln, bias=g.c_eps[0:M, 1:2]),
             reads=[g.PSB[sb]], writes=[rsB])
        P.op('act', lambda e: e.activation(out=rs[0:M, :], in_=rs[0:M, :], func=AF.Exp, scale=-0.5),
             reads=[rsB], writes=[rsB])
        qn, qnB = qnr.next()
        P.op('dve', lambda e: e.scalar_tensor_tensor(out=qn[0:M, :], in0=g.PS[bank][0:M, :], scalar=gcol[0:M, 0:1],
                                                     in1=rs[0:M, :], op0=ALU.mult, op1=ALU.mult),
             reads=[g.PSB[bank], rsB, gB], writes=[qnB])
        return qn, qnB

    def rope_mm(qn, qnB, M, rm, cs, csB, tg):
        rb = 6 + (ri[0] % 2)
        ri[0] += 1
        P.op('pe', lambda e: e.matmul(g.PS[rb][0:M, :], lhsT=rm[0:M, 0:M], rhs=qn[0:M, :], start=True, stop=True),
             reads=[qnB], writes=[g.PSB[rb]])
        t1, t1B = t1r.next()
        t2, t2B = t2r.next()
        P.op('pool', lambda e: e.tensor_tensor(out=t1[0:M, :], in0=qn[0:M, :], in1=cs[0][0:M, tg * 512:(tg + 1) * 512],
                                               op=ALU.mult), reads=[qnB, csB], writes=[t1B])
        P.op('dve', lambda e: e.tensor_tensor(out=t2[0:M, :], in0=g.PS[rb][0:M, :],
                                              in1=cs[1][0:M, tg * 512:(tg + 1) * 512], op=ALU.mult),
             reads=[g.PSB[rb], csB], writes=[t2B])
        ob, obB = obr.next()
        P.op('pool', lambda e: e.tensor_tensor(out=ob[0:M, :], in0=t1[0:M, :], in1=t2[0:M, :], op=ALU.add),
             reads=[t1B, t2B], writes=[obB])
        return ob, obB


    tiles = []

    def dma_out_feat(it, M, tg, ob, obB):
        r0 = it['row']
        P.dma('sp', g.zT[r0:r0 + M, tg * 512:(tg + 1) * 512], ob[0:M, :], reads=[obB], writes=[g.zTB])

    def mk_feat(wv, wB, it, tg, pre):
        kind = it['kind']
        M = it['n']
        st = {}

        def s0():
            if pre is not None:
                pre()
            bank = zbank()
            st['bank'] = bank
            proj_feat(wv, wB, it['off'], M, tg, bank)
            if kind == 'qk':
                sq, sqB = sqr.next()
                P.op('act', lambda e: e.activation(out=sq[0:M, :], in_=g.PS[bank][0:M, :], func=AF.Square),
                     reads=[g.PSB[bank]], writes=[sqB])
                st['sq'] = (sq, sqB)

        def s1():
            bank = st['bank']
            if kind == 'qk':
                sq, sqB = st['sq']
                sb = 4 + (si[0] % 2)
                si[0] += 1
                P.op('pe', lambda e: e.matmul(g.PS[sb][0:M, :], lhsT=g.ones[0:M, 0:M], rhs=sq[0:M, :], start=True,
                                              stop=True), reads=[sqB], writes=[g.PSB[sb]])
                rs, rsB = rsr.next()
                P.op('act', lambda e: e.activation(out=rs[0:M, :], in_=g.PS[sb][0:M, :], func=AF.Ln,
                                                   bias=g.c_eps[0:M, 1:2]), reads=[g.PSB[sb]], writes=[rsB])
                P.op('act', lambda e: e.activation(out=rs[0:M, :], in_=rs[0:M, :], func=AF.Exp, scale=-0.5),
                     reads=[rsB], writes=[rsB])
                qn, qnB = qnr.next()
                gcol, gB = it['gcol'], it['gB']
                P.op('dve', lambda e: e.scalar_tensor_tensor(out=qn[0:M, :], in0=g.PS[bank][0:M, :],
                                                             scalar=gcol[0:M, 0:1], in1=rs[0:M, :], op0=ALU.mult,
                                                             op1=ALU.mult),
                     reads=[g.PSB[bank], rsB, gB], writes=[qnB])
                st['qn'] = (qn, qnB)
                if it.get('rope') is None:
                    dma_out_feat(it, M, tg, qn, qnB)
            elif kind == 'rope':
                qn, qnB = qnr.next()
                P.op('act', lambda e: e.activation(out=qn[0:M, :], in_=g.PS[bank][0:M, :], func=AF.Copy,
                                                   scale=float(it.get('scale', 1.0))),
                     reads=[g.PSB[bank]], writes=[qnB])
                st['qn'] = (qn, qnB)
            elif kind == 'gate':
                ob, obB = obr.next()
                P.op('act', lambda e: e.activation(out=ob[0:M, :], in_=g.PS[bank][0:M, :], func=AF.Silu),
                     reads=[g.PSB[bank]], writes=[obB])
                dma_out_feat(it, M, tg, ob, obB)
            elif kind == 'fox_f':
                dst, dB = it['sb_dst']
                fb = it['fbias']
                P.op('act', lambda e: e.activation(out=dst[0:M, tg * 512:(tg + 1) * 512], in_=g.PS[bank][0:M, :],
                                                   func=AF.Exp, scale=-1.0, bias=fb[0:M, 0:1]),
                     reads=[g.PSB[bank], it['fbB']], writes=[dB])
                P.op('act', lambda e: e.activation(out=dst[0:M, tg * 512:(tg + 1) * 512],
                                                   in_=dst[0:M, tg * 512:(tg + 1) * 512], func=AF.Ln,
                                                   bias=g.c_one[0:M, 0:1]), reads=[dB], writes=[dB])
            else:
                raise ValueError(kind)

        def s2():
            if it.get('rope') is not None and kind in ('qk', 'rope'):
                qn, qnB = st['qn']
                rm, cs, csB = it['rope']
                ob, obB = rope_mm(qn, qnB, M, rm, cs, csB, tg)
                dma_out_feat(it, M, tg, ob, obB)

        return [s0, s1, s2]

    def mk_pair(wv, wB, it, tg, pre):
        cs, csB = it['cs']
        sc = float(it.get('scale', 1.0))
        st = {}
        cosv = cs[0][:, tg * 512:(tg + 1) * 512]
        sinv = cs[1][:, tg * 512:(tg + 1) * 512]

        def s0():
            if pre is not None:
                pre()
            b0 = zbank()
            proj_feat(wv, wB, it['off'], 128, tg, b0)
            b1 = zbank()
            proj_feat(wv, wB, it['off'] + 128, 128, tg, b1)
            st['b'] = (b0, b1)

        def s1():
            b0, b1 = st['b']
            x1, x1B = t1r.next()
            x2, x2B = t2r.next()
            P.op('act', lambda e: e.activation(out=x1, in_=g.PS[b0], func=AF.Copy, scale=sc), reads=[g.PSB[b0]],
                 writes=[x1B])
            P.op('act', lambda e: e.activation(out=x2, in_=g.PS[b1], func=AF.Copy, scale=sc), reads=[g.PSB[b1]],
                 writes=[x2B])
            st['x'] = (x1, x1B, x2, x2B)

        def s2():
            x1, x1B, x2, x2B = st['x']
            a1, a1B = rsr.next()
            a2, a2B = rsr.next()
            P.op('dve', lambda e: e.tensor_tensor(out=a1, in0=x1, in1=cosv, op=ALU.mult), reads=[x1B, csB], writes=[a1B])
            P.op('pool', lambda e: e.tensor_tensor(out=a2, in0=x2, in1=sinv, op=ALU.mult), reads=[x2B, csB], writes=[a2B])
            o1, o1B = obr.next()
            P.op('dve', lambda e: e.tensor_tensor(out=o1, in0=a1, in1=a2, op=ALU.subtract), reads=[a1B, a2B],
                 writes=[o1B])
            P.op('pool', lambda e: e.tensor_tensor(out=a1, in0=x1, in1=sinv, op=ALU.mult), reads=[x1B, csB, o1B],
                 writes=[a1B])
            P.op('dve', lambda e: e.tensor_tensor(out=a2, in0=x2, in1=cosv, op=ALU.mult), reads=[x2B, csB, o1B],
                 writes=[a2B])
            o2, o2B = obr.next()
            P.op('pool', lambda e: e.tensor_tensor(out=o2, in0=a1, in1=a2, op=ALU.add), reads=[a1B, a2B], writes=[o2B])
            r0 = it['row']
            P.dma('sp', g.zT[r0:r0 + 128, tg * 512:(tg + 1) * 512], o1, reads=[o1B], writes=[g.zTB])
            P.dma('sp', g.zT[r0 + 128:r0 + 256, tg * 512:(tg + 1) * 512], o2, reads=[o2B], writes=[g.zTB])

        return [s0, s1, s2]

    def mk_tok(wv, wB, it, tt, pre):
        n = it['n']
        st = {}

        def s0():
            if pre is not None:
                pre()
            bank = zbank()
            st['bank'] = bank
            for kc in range(KC):
                P.op('pe', lambda e, kc=kc: e.matmul(g.PS[bank][:, 0:n],
                                                     lhsT=g.bigT[:, kc * T + tt * 128: kc * T + (tt + 1) * 128],
                                                     rhs=wv[:, kc, it['off']:it['off'] + n], start=(kc == 0),
                                                     stop=(kc == KC - 1)),
                     reads=[wB, g.bigB], writes=[g.PSB[bank]])

        def s1():
            bank = st['bank']
            if it.get('sb_dst') is not None:
                dst, dB = it['sb_dst']
                P.op('act', lambda e: e.copy(out=dst[:, tt * n:(tt + 1) * n], in_=g.PS[bank][:, 0:n]),
                     reads=[g.PSB[bank]], writes=[dB])
            else:
                ob, obB = obr.next()
                P.op('act', lambda e: e.copy(out=ob[:, 0:n], in_=g.PS[bank][:, 0:n]), reads=[g.PSB[bank]], writes=[obB])
                vc = it['vcol']
                P.dma('sp', g.vtok[tt * 128:(tt + 1) * 128, vc:vc + n], ob[:, 0:n], reads=[obB], writes=[g.vtB])

        return [s0, s1, None]

    pres = []
    for bi, blk in enumerate(blocks):
        col0, ncols, items = blk[0], blk[1], blk[2]
        loader = blk[3] if len(blk) > 3 else None
        holder = {}

        def pre(col0=col0, ncols=ncols, loader=loader, holder=holder):
            wbf, wB = wr.next()
            wv = wbf.rearrange('p (k n) -> p k n', k=KC)
            holder['wv'], holder['wB'] = wv, wB
            if loader is not None:
                loader(wv, wB)
            else:
                P.dma('pool', wv[:, :, 0:ncols], w_in[:, col0:col0 + ncols].rearrange('(k p) n -> p k n', p=128),
                      writes=[wB])

        pres.append(pre)
        first = True
        for it in items:
            reps = NT if it['kind'] == 'tok' else 4
            for r in range(reps):
                def lazy(it=it, r=r, holder=holder, pre_fn=((lambda bi=bi: pres[bi + 1]() if bi + 1 < len(pres) else None) if first else None)):
                    built = {}

                    def s0():
                        if pre_fn is not None:
                            pre_fn()
                        wv, wB = holder['wv'], holder['wB']
                        if it['kind'] == 'tok':
                            built['s'] = mk_tok(wv, wB, it, r, None)
                        elif it['kind'] == 'rope_pair':
                            built['s'] = mk_pair(wv, wB, it, r, None)
                        else:
                            built['s'] = mk_feat(wv, wB, it, r, None)
                        built['s'][0]()

                    def s1():
                        built['s'][1]()

                    def s2():
                        if built['s'][2] is not None:
                            built['s'][2]()

                    return [s0, s1, s2]
                tiles.append(lazy())
                first = False
    pres[0]()
    nt_ = len(tiles)
    for step in range(nt_ + 2):
        if step < nt_:
            tiles[step][0]()
        if 0 <= step - 1 < nt_:
            tiles[step - 1][1]()
        if 0 <= step - 2 < nt_:
            tiles[step - 2][2]()

    P.barrier()
    A.release()


def flush_att(g):
    pend = getattr(g, 'att_pending', None)
    if pend is not None:
        g.att_pending = None
        pend()


def attend(g, N, qch, qB, ktiles, n_dv, Sb, Ob, Lb, ptr, onesL=True, finish=None):
    P = g.P
    n = len(ktiles)
    sbase = getattr(g, 'scnt', 0)
    g.scnt = sbase + n

    def emit_S(i):
        kt = ktiles[i]
        bank = Sb[(sbase + i) % 2]
        c0 = kt['c0']
        mms = []
        for kc, (kl, qc) in enumerate(zip(kt['k'], qch)):
            mms.append((kl, qc[:, c0:N], c0, N, list(kt['kB']) + list(qB)))
        for (bl, br, lo, hi, bb) in kt.get('bias', []):
            mms.append((bl, br, lo, hi, list(bb)))
        for j, (lt, rh, lo, hi, bb) in enumerate(mms):
            P.op('pe', lambda e, lt=lt, rh=rh, lo=lo, hi=hi, j=j, bank=bank, last=(j == len(mms) - 1): e.matmul(
                g.PS[bank][:, lo:hi], lhsT=lt, rhs=rh, start=(j == 0), stop=last),
                 reads=bb, writes=[g.PSB[bank]])

    def emit_P(i):
        kt = ktiles[i]
        bank = Sb[(sbase + i) % 2]
        c0 = kt['c0']
        pt, ptB = ptr.next()
        eng, fn, rd = kt['pt'](g.PS[bank][:, c0:N], pt[:, c0:N], c0)
        P.op(eng, fn, reads=[g.PSB[bank]] + list(rd), writes=[ptB])
        return pt, ptB

    def emit_PV(i, pt, ptB):
        kt = ktiles[i]
        c0 = kt['c0']
        for j in range(n_dv):
            P.op('pe', lambda e, j=j, c0=c0, pt=pt, kt=kt: e.matmul(g.PS[Ob[j]][:, c0:N], lhsT=kt['v'][j],
                                                                     rhs=pt[:, c0:N], start=(i == 0), stop=(i == n - 1)),
                 reads=[ptB] + list(kt['vB']), writes=[g.PSB[Ob[j]]])
        if onesL:
            P.op('pe', lambda e, c0=c0, pt=pt: e.matmul(g.PS[Lb][:, c0:N], lhsT=g.ones, rhs=pt[:, c0:N],
                                                        start=(i == 0), stop=(i == n - 1)),
                 reads=[ptB], writes=[g.PSB[Lb]])

    emit_S(0)
    first = emit_P(0)
    flush_att(g)
    for i in range(n):
        pt, ptB = first if i == 0 else emit_P(i)
        if i + 1 < n:
            emit_S(i + 1)
        if i < n - 1:
            emit_PV(i, pt, ptB)
        else:
            def tail(i=i, pt=pt, ptB=ptB):
                emit_PV(i, pt, ptB)
                if finish is not None:
                    finish()
            g.att_pending = tail


def pt_exp(bias_ap=None, biasB=()):
    def f(S, PTv, c0):
        if bias_ap is None:
            return 'act', (lambda e: e.activation(out=PTv, in_=S, func=AF.Exp)), []
        return 'act', (lambda e: e.activation(out=PTv, in_=S, func=AF.Exp, bias=bias_ap)), list(biasB)
    return f


def softmax_finish(g, N, Obank, Lbank, gate_ap, gateB, dst_ap, rlr, tmr):
    P = g.P
    rl, rlB = rlr.next()
    P.op('dve', lambda e: e.reciprocal(out=rl[:, 0:N], in_=g.PS[Lbank][:, 0:N]), reads=[g.PSB[Lbank]], writes=[rlB])
    tm, tmB = tmr.next()
    P.op('dve', lambda e: e.tensor_tensor(out=tm[:, 0:N], in0=g.PS[Obank][:, 0:N], in1=rl[:, 0:N], op=ALU.mult),
         reads=[g.PSB[Obank], rlB], writes=[tmB])
    P.op('pool', lambda e: e.tensor_tensor(out=dst_ap, in0=tm[:, 0:N], in1=gate_ap, op=ALU.mult),
         reads=[tmB] + list(gateB), writes=[g.bigB])


def phase_C(g, w_out, nchunks, h_src, hbufs_src, h_dst, hbufs_dst, y_prev=None, y_out=None):
    P, A = g.P, g.A
    A.mark()
    wr = Ring(A, 'wo', 2, nchunks * 512, BF16)
    hr = Ring(A, 'hres', 3, 512, F32)
    yr = Ring(A, 'yprev', 2, 512, F32)
    zi = 0
    for cg in range(4):
        wbf, wB = wr.next()
        wv = wbf.rearrange('p (k n) -> p k n', k=nchunks)
        P.dma('pool', wv, w_out[:, cg * 512:(cg + 1) * 512].rearrange('(k p) n -> p k n', p=128), writes=[wB])
        for tt in range(NT):
            bank = zi % 4
            zi += 1
            ht, hB = hr.next()
            if y_out is None:
                P.dma('sp', ht, h_src[tt * 128:(tt + 1) * 128, cg * 512:(cg + 1) * 512], reads=[hbufs_src[tt]],
                      writes=[hB])
            if y_prev is not None:
                yp, ypB = yr.next()
                P.dma('sp', yp, y_prev[tt * 128:(tt + 1) * 128, cg * 512:(cg + 1) * 512], reads=[g.ypB], writes=[ypB])
            for c in range(nchunks):
                P.op('pe', lambda e, c=c, bank=bank, tt=tt, wv=wv: e.matmul(
                    g.PS[bank], lhsT=g.bigT[:, c * T + tt * 128: c * T + (tt + 1) * 128], rhs=wv[:, c, :],
                    start=(c == 0), stop=(c == nchunks - 1)), reads=[wB, g.bigB], writes=[g.PSB[bank]])
            if y_out is not None:
                P.op('act', lambda e, ht=ht, bank=bank: e.copy(out=ht, in_=g.PS[bank]), reads=[g.PSB[bank]], writes=[hB])
                P.dma('sp', y_out[tt * 128:(tt + 1) * 128, cg * 512:(cg + 1) * 512], ht, reads=[hB], writes=[g.ypB])
            else:
                P.op('dve', lambda e, ht=ht, bank=bank: e.tensor_tensor(out=ht, in0=g.PS[bank], in1=ht, op=ALU.add),
                     reads=[g.PSB[bank], hB], writes=[hB])
                if y_prev is not None:
                    P.op('pool', lambda e, ht=ht, yp=yp: e.tensor_tensor(out=ht, in0=ht, in1=yp, op=ALU.add),
                         reads=[hB, ypB], writes=[hB])
                P.dma('sp', h_dst[tt * 128:(tt + 1) * 128, cg * 512:(cg + 1) * 512], ht, reads=[hB],
                      writes=[hbufs_dst[tt]])
    P.barrier()
    A.release()


def load_col(g, A, src_row_ap, n, mult, name):
    P = g.P
    col = A.alloc(1, F32)
    B = Buf(name)
    P.dma('sp', col[0:n, :], src_row_ap.rearrange('o d -> d o'), writes=[B])
    P.op('pool', lambda e: e.tensor_scalar(out=col[0:n, :], in0=col[0:n, :], scalar1=float(mult), scalar2=None,
                                           op0=ALU.mult), reads=[B], writes=[B])
    return col, B


def load_rope(g, A, tab, name):
    P = g.P
    cs = [A.alloc(T, F32), A.alloc(T, F32)]
    B = Buf(name)
    P.dma('sp', cs[0], tab[0], writes=[B])
    P.dma('sp', cs[1], tab[1], writes=[B])
    return cs, B


def load_kv(g, A, krow0, nkc, vcol0, ndv, kdst, kB, vdst, vB, nkt=NT):
    P = g.P
    nk = nkt * 128
    P.dma('sp', kdst.rearrange('p (c t) -> p c t', c=nkc)[:, :, 0:nk],
          g.zT[krow0:krow0 + nkc * 128, 0:nk].rearrange('(c p) t -> p c t', p=128), reads=[g.zTB], writes=[kB])
    P.dma('sp', vdst.rearrange('p (k d) -> p k d', k=NT)[:, 0:nkt, :],
          g.vtok[0:nk, vcol0:vcol0 + ndv * 128].rearrange('(k p) d -> p k d', p=128), reads=[g.vtB], writes=[vB])


def softmax_finish_act(g, N, Obank, Lbank, gate_ap, gateB, dst_ap, rlr, tmr):
    P = g.P
    tm, tmB = tmr.next()
    P.op('act', lambda e: e.copy(out=tm[:, 0:N], in_=g.PS[Obank][:, 0:N]), reads=[g.PSB[Obank]], writes=[tmB])
    rl, rlB = rlr.next()
    P.op('act', lambda e: e.activation(out=rl[:, 0:N], in_=g.PS[Lbank][:, 0:N], func=AF.Ln), reads=[g.PSB[Lbank]],
         writes=[rlB])
    P.op('act', lambda e: e.activation(out=rl[:, 0:N], in_=rl[:, 0:N], func=AF.Exp, scale=-1.0), reads=[rlB],
         writes=[rlB])
    P.op('pool', lambda e: e.tensor_tensor(out=tm[:, 0:N], in0=tm[:, 0:N], in1=rl[:, 0:N], op=ALU.mult),
         reads=[tmB, rlB], writes=[tmB])
    P.op('pool', lambda e: e.tensor_tensor(out=dst_ap, in0=tm[:, 0:N], in1=gate_ap, op=ALU.mult),
         reads=[tmB] + list(gateB), writes=[g.bigB])


TP = 2
NH = 16 // TP
NKV = 4 // TP
RH = 8 // TP
GROUPS = [[0, 1], [2, 3], [4, 5], [6, 7]]


class OutProj:
    def __init__(self, g, w_out, nchunks, banks, hsrc, oset, pool_path=False, nbuf=2, scratch=None):
        P, A = g.P, g.A
        self.g, self.nchunks, self.banks = g, nchunks, banks
        self.hsrc, self.oset, self.pool_path = hsrc, oset, pool_path
        self.tmr = Ring(A, 'ctmp', 2, 512, F32) if pool_path else None
        wo = A.alloc(nchunks * D, BF16)
        self.wv = wo.rearrange('p (k n) -> p k n', k=nchunks)
        self.woB = [Buf('wo%d' % i) for i in range(4)]
        for cg in range(4):
            P.dma('pool', self.wv[:, :, cg * 512:(cg + 1) * 512],
                  w_out[:, cg * 512:(cg + 1) * 512].rearrange('(k p) n -> p k n', p=128), writes=[self.woB[cg]])
        if scratch is not None:
            nb = scratch.shape[1] // D
            self.yr = Ring.__new__(Ring)
            self.yr.aps = [scratch[:, i * D:(i + 1) * D] for i in range(nb)]
            self.yr.bufs = [Buf('ytS%d' % i) for i in range(nb)]
            self.yr.i = 0
        else:
            self.yr = Ring(A, 'yt', nbuf, D, F32)
        self.zi = 0
        self.pref = {}

    def prefetch(self, j):
        if len(self.yr.aps) < 4 or j in self.pref:
            return
        P = self.g.P
        tiles = []
        for tt in range(4 * j, 4 * j + 4):
            yt, ytB = self.yr.next()
            hap, hB_ = self.hsrc(tt)
            P.dma('sp', yt, hap, reads=[hB_], writes=[ytB])
            tiles.append((yt, ytB))
        self.pref[j] = tiles

    def chunk(self, j):
        g = self.g
        P = g.P
        flush_att(g)
        nchunks, wv, woB = self.nchunks, self.wv, self.woB
        oset = self.oset
        pre = len(self.yr.aps) >= 4
        if pre and j not in self.pref:
            self.prefetch(j)
        tiles = self.pref.pop(j) if pre else []
        for ti, tt in enumerate(range(4 * j, 4 * j + 4)):
            if pre:
                yt, ytB = tiles[ti]
            else:
                yt, ytB = self.yr.next()
                hap, hB_ = self.hsrc(tt)
                P.dma('sp', yt, hap, reads=[hB_], writes=[ytB])
            if self.pool_path:
                P.op('pool', lambda e, yt=yt: e.tensor_scalar(out=yt, in0=yt, scalar1=0.5, scalar2=None, op0=ALU.mult),
                     reads=[ytB], writes=[ytB])
            for cg in range(4):
                bank = self.banks[self.zi % len(self.banks)]
                self.zi += 1
                for c in range(nchunks):
                    P.op('pe', lambda e, c=c, bank=bank, tt=tt, cg=cg: e.matmul(
                        g.PS[bank], lhsT=g.bigT[:, c * T + tt * 128: c * T + (tt + 1) * 128],
                        rhs=wv[:, c, cg * 512:(cg + 1) * 512], start=(c == 0), stop=(c == nchunks - 1)),
                         reads=[woB[cg], g.bigB], writes=[g.PSB[bank]])
                ysl = yt[:, cg * 512:(cg + 1) * 512]
                if self.pool_path:
                    tm, tmB = self.tmr.next()
                    P.op('act', lambda e, tm=tm, bank=bank: e.copy(out=tm, in_=g.PS[bank]), reads=[g.PSB[bank]],
                         writes=[tmB])
                    P.op('pool', lambda e, ysl=ysl, tm=tm: e.tensor_tensor(out=ysl, in0=ysl, in1=tm, op=ALU.add),
                         reads=[tmB, ytB], writes=[ytB])
                else:
                    P.op('dve', lambda e, ysl=ysl, bank=bank: e.scalar_tensor_tensor(
                        out=ysl, in0=ysl, scalar=0.5, in1=g.PS[bank], op0=ALU.mult, op1=ALU.add),
                         reads=[g.PSB[bank], ytB], writes=[ytB])
            P.dma('pool', g.ypart[j][(tt % 4) * 128:(tt % 4 + 1) * 128, :], yt, reads=[ytB], writes=[g.ypB[j]])
        P.coll(lambda e, j=j: e.collective_compute("AllReduce", ALU.add, replica_groups=GROUPS,
                                                   ins=[g.ypart_t[j].ap().opt()],
                                                   outs=[g.ysum_t[oset][j].ap().opt()]),
               reads=[g.ypB[j]], writes=[g.ysB[oset][j]])


def final_copy(g, oset, out, obufs):
    P = g.P
    for j in range(4):
        P.dma('sp', out[j * 512:(j + 1) * 512, :], g.ysum[oset][j], reads=[g.ysB[oset][j]], writes=[obufs[j]])


QC0, KC0, VC0, GC0 = 0, NH * 128, NH * 128 + NKV * 128, NH * 128 + 2 * NKV * 128
IC0 = GC0 + NH * 128
KR0, GR0 = NH * 128, NH * 128 + NKV * 128
IR0 = GR0 + NH * 128


def gqa_blocks(gq, gqB, gk, gkB, rope):
    blocks = []
    for b in range(NH // 4):
        blocks.append((QC0 + b * 512, 512, [dict(kind='qk', off=j * 128, n=128, row=(b * 4 + j) * 128, gcol=gq, gB=gqB,
                                                 rope=rope) for j in range(4)]))
    blocks.append((KC0, NKV * 128, [dict(kind='qk', off=j * 128, n=128, row=KR0 + j * 128, gcol=gk, gB=gkB, rope=rope)
                                    for j in range(NKV)]))
    blocks.append((VC0, NKV * 128, [dict(kind='tok', off=0, n=NKV * 128, vcol=0)]))
    for b in range(NH // 4):
        blocks.append((GC0 + b * 512, 512, [dict(kind='gate', off=j * 128, n=128, row=GR0 + (b * 4 + j) * 128)
                                            for j in range(4)]))
    return blocks


def gqa_attention_tile(g, qt, gi, kT, kB, vS, vB, q4r, g4r, ptr, rlr, tmr, bias_fn, Ob, Lb, pre_fn=None):
    P = g.P
    q4, q4B = q4r.next()
    g4, g4B = g4r.next()
    P.dma('sp', q4.rearrange('p (r q) -> p r q', r=4),
          g.zT[gi * 512:(gi + 1) * 512, qt * 128:(qt + 1) * 128].rearrange('(r p) q -> p r q', p=128),
          reads=[g.zTB], writes=[q4B])
    P.dma('sp', g4.rearrange('p (r q) -> p r q', r=4),
          g.zT[GR0 + gi * 512:GR0 + (gi + 1) * 512, qt * 128:(qt + 1) * 128].rearrange('(r p) q -> p r q', p=128),
          reads=[g.zTB], writes=[g4B])
    if pre_fn is not None:
        pre_fn(q4, q4B)
    kts = []
    for kt in range(qt + 1):
        bias = []
        for (bl, br, bb) in bias_fn(kt):
            for r in range(4):
                bias.append((bl, br, r * 128, (r + 1) * 128, bb))
        kts.append(dict(k=[kT[:, gi * T + kt * 128: gi * T + (kt + 1) * 128]], kB=[kB],
                        v=[vS[:, kt * NKV * 128 + gi * 128: kt * NKV * 128 + (gi + 1) * 128]], vB=[vB], c0=0, bias=bias,
                        pt=pt_exp()))
    dst = g.bigT[:, gi * 4 * T:(gi * 4 + 4) * T].rearrange('p (r t) -> p r t', r=4)[:, :, qt * 128:(qt + 1) * 128]
    tmv = lambda ap: ap.rearrange('p (r q) -> p r q', r=4)

    def fin():
        tm, tmB = tmr.next()
        P.op('act', lambda e: e.copy(out=tm, in_=g.PS[Ob]), reads=[g.PSB[Ob]], writes=[tmB])
        rl, rlB = rlr.next()
        P.op('act', lambda e: e.activation(out=rl, in_=g.PS[Lb], func=AF.Ln), reads=[g.PSB[Lb]], writes=[rlB])
        P.op('act', lambda e: e.activation(out=rl, in_=rl, func=AF.Exp, scale=-1.0), reads=[rlB], writes=[rlB])
        P.op('pool', lambda e: e.tensor_tensor(out=tm, in0=tm, in1=rl, op=ALU.mult), reads=[tmB, rlB], writes=[tmB])
        P.op('pool', lambda e: e.tensor_tensor(out=dst, in0=tmv(tm), in1=tmv(g4), op=ALU.mult), reads=[tmB, g4B],
             writes=[g.bigB])

    attend(g, 512, [q4], [q4B], kts, 1, [0, 1], [Ob], Lb, ptr, finish=fin)


def layer_dsa(g, prm):
    P, A = g.P, g.A
    A.mark()
    wi_sb = A.alloc(NT * 16, F32)
    wiB = Buf('wi')
    A.mark()
    cs128, cs128B = load_rope(g, A, g.rope128, 'r128')
    cs64, cs64B = load_rope(g, A, g.rope64, 'r64')
    gq, gqB = load_col(g, A, prm['qn'], 128, 1.0, 'gq')
    gk, gkB = load_col(g, A, prm['kn'], 128, float(np.sqrt(128.0)), 'gk')
    blocks = gqa_blocks(gq, gqB, gk, gkB, (g.rm128, cs128, cs128B))
    for b in range(2):
        blocks.append((IC0 + b * 512, 512, [dict(kind='rope', off=j * 128, n=128, row=IR0 + (b * 4 + j) * 128,
                                                 rope=(g.rm64, cs64, cs64B)) for j in range(4)]))
    blocks.append((IC0 + 1024, 80, [dict(kind='rope', off=0, n=64, row=IR0 + 1024, rope=(g.rm64, cs64, cs64B)),
                                    dict(kind='tok', off=64, n=16, sb_dst=(wi_sb, wiB))]))
    phase_A2(g, prm['w_in'], blocks)
    A.release()
    A.mark()
    kT = A.alloc(NKV * T, BF16)
    vS = A.alloc(NKV * NT * 128, BF16)
    ki2 = A.alloc(T, BF16)
    kB, vB, kiB = Buf('kT'), Buf('vS'), Buf('ki2')
    load_kv(g, A, KR0, NKV, 0, NKV, kT, kB, vS, vB)
    P.dma('sp', ki2[0:64, :], g.zT[IR0 + 1024:IR0 + 1088, :], reads=[g.zTB], writes=[kiB])
    P.dma('sp', ki2[64:128, :], g.zT[IR0 + 1024:IR0 + 1088, :], reads=[g.zTB], writes=[kiB])
    q4r = Ring(A, 'q4', 2, 512, BF16)
    g4r = Ring(A, 'g4', 2, 512, BF16)
    qir = Ring(A, 'qi', 2, 8 * 128, BF16)
    accr = Ring(A, 'acc', 2, T, F32)
    rrr = Ring(A, 'rr', 3, 512, BF16)
    dgr = Ring(A, 'dg', 4, 128, BF16)
    jkr = Ring(A, 'junk', 2, T, BF16)
    nsr = Ring(A, 'ns', 2, T, BF16)
    mtr = Ring(A, 'maskT', 4, T, BF16)
    ptr = Ring(A, 'pt', 3, 512, BF16)
    rlr = Ring(A, 'rl', 2, 512, F32)
    tmr = Ring(A, 'tm', 2, 512, F32)
    smr = Ring(A, 'bis', 2, 40, F32)
    oproj = OutProj(g, prm['w_out'], NH, [4, 5, 6, 7], prm['hsrc'], prm['oset'], scratch=g.bigT[:, NH * T:KC * T].bitcast(F32))
    p2 = A.alloc(32, F32)
    p2B = Buf('p2')
    NBIS = 16
    for i in range(NBIS):
        P.op('pool', lambda e, i=i: e.memset(p2[:, i:i + 1], float(2.0 ** (-i))), writes=[p2B])
    masks = {}

    def indexer_pair(qts):
        mem = []
        for qt in qts:
            L = (qt + 1) * 128
            qi, qiB = qir.next()
            P.dma('sp', qi.rearrange('p (a q) -> p a q', a=8),
                  g.zT[IR0:IR0 + 1024, qt * 128:(qt + 1) * 128].rearrange('(a p) q -> p a q', p=128), reads=[g.zTB],
                  writes=[qiB])
            acc, accB = accr.next()
            nch = (L + 511) // 512
            cnt = 0
            for c in range(nch):
                ncol = min(512, L - c * 512)
                pend = None
                for h in range(17):
                    if h < 16:
                        pair, half = divmod(h, 2)
                        rb = 4 + (cnt % 2)
                        cnt += 1
                        P.op('pe', lambda e, rb=rb, ncol=ncol, half=half, pair=pair, c=c, qi=qi: e.matmul(
                            g.PS[rb][:, 0:ncol], lhsT=qi[half * 64:(half + 1) * 64, pair * 128:(pair + 1) * 128],
                            rhs=ki2[half * 64:(half + 1) * 64, c * 512:c * 512 + ncol], start=True, stop=True),
                             reads=[qiB, kiB], writes=[g.PSB[rb]])
                        rr, rrB = rrr.next()
                        P.op('act', lambda e, rr=rr, rb=rb, ncol=ncol: e.activation(out=rr[:, 0:ncol],
                                                                                   in_=g.PS[rb][:, 0:ncol],
                                                                                   func=AF.Relu),
                             reads=[g.PSB[rb]], writes=[rrB])
                        dg, dgB = dgr.next()
                        wcol = wi_sb[:, qt * 16 + h: qt * 16 + h + 1]
                        P.op('dve', lambda e, dg=dg, wcol=wcol: e.tensor_scalar(out=dg, in0=g.ident, scalar1=wcol,
                                                                                scalar2=None, op0=ALU.mult),
                             reads=[wiB], writes=[dgB])
                        cur = (dg, dgB, rr, rrB, h)
                    else:
                        cur = None
                    if pend is not None:
                        pdg, pdgB, prr, prrB, ph = pend
                        P.op('pe', lambda e, pdg=pdg, prr=prr, ncol=ncol, ph=ph: e.matmul(
                            g.PS[6][:, 0:ncol], lhsT=pdg, rhs=prr[:, 0:ncol], start=(ph == 0), stop=(ph == 15)),
                             reads=[pdgB, prrB], writes=[g.PSB[6]])
                    pend = cur
                P.op('act', lambda e, acc=acc, c=c, ncol=ncol: e.copy(out=acc[:, c * 512:c * 512 + ncol],
                                                                      in_=g.PS[6][:, 0:ncol]),
                     reads=[g.PSB[6]], writes=[accB])
            sm, smB = smr.next()
            jk, jkB = jkr.next()
            mem.append(dict(qt=qt, L=L, acc=acc, accB=accB, sm=sm, smB=smB, jk=jk, jkB=jkB))
        for m in mem:
            sm, smB, acc, accB, L, qt = m['sm'], m['smB'], m['acc'], m['accB'], m['L'], m['qt']
            P.op('dve', lambda e, sm=sm, acc=acc, L=L: e.tensor_reduce(out=sm[:, 0:1], in_=acc[:, 0:L], axis=AX.X,
                                                                       op=ALU.max, apply_absolute_value=True),
                 reads=[accB], writes=[smB])
            P.op('dve', lambda e, sm=sm: e.tensor_scalar(out=sm[:, 0:1], in0=sm[:, 0:1], scalar1=1.0001, scalar2=1e-6,
                                                         op0=ALU.mult, op1=ALU.add), reads=[smB], writes=[smB])
            P.op('dve', lambda e, sm=sm: e.tensor_scalar(out=sm[:, 1:2], in0=sm[:, 0:1], scalar1=-1.0, scalar2=None,
                                                         op0=ALU.mult), reads=[smB], writes=[smB])
            P.op('dve', lambda e, sm=sm: e.tensor_scalar(out=sm[:, 8:8 + NBIS], in0=p2[:, 0:NBIS], scalar1=sm[:, 0:1],
                                                         scalar2=None, op0=ALU.mult), reads=[smB, p2B], writes=[smB])
            P.op('dve', lambda e, acc=acc, qt=qt: e.tensor_tensor(out=acc[:, qt * 128:(qt + 1) * 128],
                                                                  in0=acc[:, qt * 128:(qt + 1) * 128], in1=g.negtri,
                                                                  op=ALU.add), reads=[accB], writes=[accB])
        for i in range(NBIS):
            for m in mem:
                sm, smB = m['sm'], m['smB']
                P.op('dve', lambda e, sm=sm, i=i: e.tensor_tensor(out=sm[:, 2:3], in0=sm[:, 1:2], in1=sm[:, 8 + i:9 + i],
                                                                  op=ALU.add), reads=[smB], writes=[smB])
            for m in mem:
                sm, smB, acc, accB, L, jk, jkB = m['sm'], m['smB'], m['acc'], m['accB'], m['L'], m['jk'], m['jkB']
                P.op('dve', lambda e, sm=sm, acc=acc, L=L, jk=jk: e.tensor_scalar(
                    out=jk[:, 0:L], in0=acc[:, 0:L], scalar1=sm[:, 2:3], scalar2=None, op0=ALU.is_ge, op1=ALU.add,
                    accum_out=sm[:, 3:4]), reads=[accB, smB], writes=[jkB, smB])
            for m in mem:
                sm, smB = m['sm'], m['smB']
                P.op('dve', lambda e, sm=sm, i=i: e.scalar_tensor_tensor(out=sm[:, 4:5], in0=sm[:, 3:4], scalar=255.5,
                                                                         in1=sm[:, 8 + i:9 + i], op0=ALU.is_ge,
                                                                         op1=ALU.mult), reads=[smB], writes=[smB])
            for m in mem:
                sm, smB = m['sm'], m['smB']
                P.op('dve', lambda e, sm=sm: e.tensor_tensor(out=sm[:, 1:2], in0=sm[:, 1:2], in1=sm[:, 4:5], op=ALU.add),
                     reads=[smB], writes=[smB])
        return mem

    def indexer_masks(mem):
        for m in mem:
            sm, smB, acc, accB, L, qt = m['sm'], m['smB'], m['acc'], m['accB'], m['L'], m['qt']
            ns, nsB = nsr.next()
            P.op('dve', lambda e, ns=ns, acc=acc, sm=sm, L=L: e.tensor_scalar(out=ns[:, 0:L], in0=acc[:, 0:L],
                                                                              scalar1=sm[:, 1:2], scalar2=None,
                                                                              op0=ALU.is_lt),
                 reads=[accB, smB], writes=[nsB])
            mt, mtB = mtr.next()
            pb = g.PS[7].bitcast(BF16)
            for k0 in range(0, qt + 1, 8):
                k1 = min(qt + 1, k0 + 8)
                for kt in range(k0, k1):
                    P.op('pe', lambda e, kt=kt, k0=k0, ns=ns: e.transpose(out=pb[:, (kt - k0) * 128:(kt - k0 + 1) * 128],
                                                                          in_=ns[:, kt * 128:(kt + 1) * 128],
                                                                          identity=g.ident),
                         reads=[nsB], writes=[g.PSB[7]])
                P.op('act', lambda e, k0=k0, k1=k1, mt=mt: e.copy(out=mt[:, k0 * 128:k1 * 128],
                                                                  in_=pb[:, 0:(k1 - k0) * 128]),
                     reads=[g.PSB[7]], writes=[mtB])
            masks[qt] = (mt, mtB)

    pend_c = [None]

    def attn(qt):
        for gi in range(NKV):
            if qt >= 2:
                mt, mtB = masks[qt]
                bias_fn = lambda kt, mt=mt, mtB=mtB: [(g.negI, mt[:, kt * 128:(kt + 1) * 128], [mtB])]
            else:
                bias_fn = lambda kt, qt=qt: ([(g.negI, g.ctri, [])] if kt == qt else [])
            gqa_attention_tile(g, qt, gi, kT, kB, vS, vB, q4r, g4r, ptr, rlr, tmr, bias_fn, 2, 3)
            if pend_c[0] is not None:
                oproj.chunk(pend_c[0])
                pend_c[0] = None

    for k in range(NT // 2):
        mem = indexer_pair([2 * k + 2, 2 * k + 3]) if k + 1 < NT // 2 else None
        for qt in (2 * k, 2 * k + 1):
            attn(qt)
            if qt % 4 == 3:
                pend_c[0] = qt // 4
                oproj.prefetch(qt // 4)
        if mem is not None:
            indexer_masks(mem)
    if pend_c[0] is not None:
        oproj.chunk(pend_c[0])
    P.barrier(cc=False)
    A.release()
    A.release()


def layer_moba(g, prm):
    P, A = g.P, g.A
    A.mark()
    A.mark()
    cs128, cs128B = load_rope(g, A, g.rope128, 'r128')
    gq, gqB = load_col(g, A, prm['qn'], 128, 1.0, 'gq')
    gk, gkB = load_col(g, A, prm['kn'], 128, float(np.sqrt(128.0)), 'gk')
    blocks = gqa_blocks(gq, gqB, gk, gkB, (g.rm128, cs128, cs128B))
    phase_A2(g, prm['w_in'], blocks)
    A.release()
    A.mark()
    kT = A.alloc(NKV * T, BF16)
    vS = A.alloc(NKV * NT * 128, BF16)
    kB, vB = Buf('kT'), Buf('vS')
    load_kv(g, A, KR0, NKV, 0, NKV, kT, kB, vS, vB)
    km32 = A.alloc(8 * NKV, F32)
    kmb = A.alloc(8 * NKV, BF16)
    kmB = Buf('kmean')
    for gi in range(NKV):
        P.op('dve', lambda e, gi=gi: e.tensor_reduce(out=km32[:, gi * 8:(gi + 1) * 8],
                                                     in_=kT[:, gi * T:(gi + 1) * T].rearrange('p (n s) -> p n s', n=8),
                                                     axis=AX.X, op=ALU.add), reads=[kB], writes=[kmB])
    P.op('dve', lambda e: e.tensor_scalar(out=kmb, in0=km32, scalar1=1.0 / 256.0, scalar2=None, op0=ALU.mult),
         reads=[kmB], writes=[kmB])
    q4r = Ring(A, 'q4', 2, 512, BF16)
    g4r = Ring(A, 'g4', 2, 512, BF16)
    ptr = Ring(A, 'pt', 3, 512, BF16)
    rlr = Ring(A, 'rl', 2, 512, F32)
    tmr = Ring(A, 'tm', 2, 512, F32)
    gsr = Ring(A, 'gs', 2, 16, F32)
    nsr = Ring(A, 'ns', 2, 8, BF16)
    ntr = Ring(A, 'nsT', 2, 128, BF16)
    oproj = OutProj(g, prm['w_out'], NH, [6, 7], prm['hsrc'], prm['oset'], scratch=g.bigT[:, NH * T:KC * T].bitcast(F32))
    cnt = [0]
    pend_c = None
    for qt in range(NT):
        own = qt // 2
        for gi in range(NKV):
            sel = own > 3
            state = {}

            def pre_fn(q4, q4B, gi=gi, own=own, state=state):
                for r in range(4):
                    P.op('pe', lambda e, r=r: e.matmul(g.PS[7][:, 0:8], lhsT=q4[:, r * 128:(r + 1) * 128],
                                                       rhs=kmb[:, gi * 8:(gi + 1) * 8], start=(r == 0), stop=(r == 3)),
                         reads=[q4B, kmB], writes=[g.PSB[7]])
                gs, gsB = gsr.next()
                P.op('dve', lambda e: e.tensor_copy(out=gs[:, 0:8], in_=g.PS[7][:, 0:8]), reads=[g.PSB[7]], writes=[gsB])
                P.op('dve', lambda e: e.memset(gs[:, own:8], -1e30), reads=[gsB], writes=[gsB])
                P.op('dve', lambda e: e.max(out=gs[:, 8:16], in_=gs[:, 0:8]), reads=[gsB], writes=[gsB])
                ns, nsB = nsr.next()
                P.op('dve', lambda e: e.tensor_scalar(out=ns, in0=gs[:, 0:8], scalar1=gs[:, 10:11], scalar2=None,
                                                      op0=ALU.is_lt), reads=[gsB], writes=[nsB])
                pb = g.PS[6].bitcast(BF16)
                P.op('pe', lambda e: e.transpose(out=pb[0:8, 0:128], in_=ns, identity=g.ident), reads=[nsB],
                     writes=[g.PSB[6]])
                nt, ntB = ntr.next()
                P.op('act', lambda e: e.copy(out=nt[0:8, :], in_=pb[0:8, 0:128]), reads=[g.PSB[6]], writes=[ntB])
                state['nt'] = (nt, ntB)

            def bias_fn(kt, qt=qt, own=own, sel=sel, state=state):
                if kt == qt:
                    return [(g.negI, g.ctri, [])]
                n = kt // 2
                if sel and n < own:
                    nt, ntB = state['nt']
                    return [(g.e8[0:8, n * 128:(n + 1) * 128], nt[0:8, :], [ntB])]
                return []

            k = cnt[0] % 2
            cnt[0] += 1
            gqa_attention_tile(g, qt, gi, kT, kB, vS, vB, q4r, g4r, ptr, rlr, tmr, bias_fn, 2 + k, 4 + k,
                               pre_fn=(pre_fn if sel else None))
            if pend_c is not None:
                oproj.chunk(pend_c)
                pend_c = None
        if qt % 4 == 3:
            pend_c = qt // 4
            oproj.prefetch(pend_c)
    if pend_c is not None:
        oproj.chunk(pend_c)
    P.barrier(cc=False)
    A.release()
    A.release()


def layer_ret(g, prm):
    P, A = g.P, g.A
    A.mark()
    A.mark()
    cs, csB = load_rope(g, A, g.rope256, 'r256')
    blocks = []
    QW = RH * 256
    for b in range(QW // 512):
        blocks.append((b * 512, 512, [dict(kind='rope_pair', off=i * 256, row=(b * 2 + i) * 256, cs=(cs, csB), scale=1.0)
                                      for i in range(2)]))
    for b in range(QW // 512):
        blocks.append((QW + b * 512, 512, [dict(kind='rope_pair', off=i * 256, row=QW + (b * 2 + i) * 256,
                                                cs=(cs, csB), scale=1.0 / 16.0) for i in range(2)]))
    for b in range(RH):
        blocks.append((2 * QW + b * 512, 512, [dict(kind='tok', off=0, n=512, vcol=b * 512)]))
    for b in range(RH):
        blocks.append((2 * QW + RH * 512 + b * 512, 512,
                       [dict(kind='gate', off=j * 128, n=128, row=2 * QW + (b * 4 + j) * 128) for j in range(4)]))
    phase_A2(g, prm['w_in'], blocks)
    A.release()
    A.mark()
    scl = A.alloc(RH * 16, F32)
    sclB = Buf('scl')
    P.dma('sp', scl, prm['ret_sc'], writes=[sclB])
    kr = Ring(A, 'kh', 2, 2 * T, BF16)
    vr = Ring(A, 'vh', 2, NT * 512, BF16)
    wr = Ring(A, 'rw', 2, 1024, F32)
    gnr = Ring(A, 'gn', 2, 4, F32)
    q2r = Ring(A, 'q2', 2, 1024, BF16)
    g4r = Ring(A, 'g4', 2, 2048, BF16)
    ptr = Ring(A, 'pt', 3, 512, BF16)
    obr = Ring(A, 'obr', 4, 512, BF16)
    osr = Ring(A, 'osq', 4, 512, BF16)
    mr = Ring(A, 'mean', 2, 512, F32)
    vrr = Ring(A, 'var', 2, 512, F32)
    t1r = Ring(A, 't1', 3, 512, F32)
    for hl in range(RH):
        kh, khB = kr.next()
        vh, vhB = vr.next()
        load_kv(g, A, QW + hl * 256, 2, hl * 512, 4, kh, khB, vh, vhB)
        rw, rwB = wr.next()
        P.dma('sp', rw.rearrange('p (a u) -> p a u', a=2), prm['ret_w'][hl].rearrange('a p u -> p a u'), writes=[rwB])
        gn, gnB = gnr.next()
        P.dma('sp', gn, prm['gn'][0:1, hl * 512:(hl + 1) * 512].rearrange('o (j p) -> p (o j)', p=128), writes=[gnB],
              allow_slow_non_contiguous=True)
        for G4 in range(4):
            q2, q2B = q2r.next()
            g4, g4B = g4r.next()
            P.dma('sp', q2.rearrange('p (c q) -> p c q', c=2),
                  g.zT[hl * 256:(hl + 1) * 256, G4 * 512:(G4 + 1) * 512].rearrange('(c p) q -> p c q', p=128),
                  reads=[g.zTB], writes=[q2B])
            P.dma('sp', g4.rearrange('p (j q) -> p j q', j=4),
                  g.zT[2 * QW + hl * 512:2 * QW + (hl + 1) * 512, G4 * 512:(G4 + 1) * 512].rearrange('(j p) q -> p j q', p=128),
                  reads=[g.zTB], writes=[g4B])
            kts = []
            for kt in range(4 * G4 + 4):
                m = kt - 4 * G4
                c0 = max(0, m) * 128
                if m < 0:
                    dd = 4 * G4 - kt - 1
                    scol = scl[:, hl * 16 + dd: hl * 16 + dd + 1]

                    def ptf(S, PTv, c0, scol=scol, rw=rw, rwB=rwB):
                        return 'dve', (lambda e: e.scalar_tensor_tensor(out=PTv, in0=S, scalar=scol, in1=rw[:, 512:1024],
                                                                        op0=ALU.mult, op1=ALU.mult)), [rwB, sclB]
                else:
                    def ptf(S, PTv, c0, rw=rw, rwB=rwB):
                        return 'dve', (lambda e: e.tensor_tensor(out=PTv, in0=S, in1=rw[:, 0:512 - c0], op=ALU.mult)), [rwB]
                kts.append(dict(k=[kh[:, c * T + kt * 128: c * T + (kt + 1) * 128] for c in range(2)], kB=[khB],
                                v=[vh[:, kt * 512 + j * 128: kt * 512 + (j + 1) * 128] for j in range(4)], vB=[vhB],
                                c0=c0, bias=[], pt=ptf))
            attend(g, 512, [q2[:, 0:512], q2[:, 512:1024]], [q2B], kts, 4, [0, 1], [2, 3, 4, 5], None, ptr, onesL=False)
            flush_att(g)
            obs = []
            for j in range(4):
                ob, obB = obr.next()
                os_, osB = osr.next()
                P.op('act', lambda e, ob=ob, j=j: e.copy(out=ob, in_=g.PS[2 + j]), reads=[g.PSB[2 + j]], writes=[obB])
                P.op('act', lambda e, os_=os_, j=j: e.activation(out=os_, in_=g.PS[2 + j], func=AF.Square),
                     reads=[g.PSB[2 + j]], writes=[osB])
                obs.append((ob, obB, os_, osB))
            for j in range(4):
                P.op('pe', lambda e, j=j, obs=obs: e.matmul(g.PS[6], lhsT=g.ones, rhs=obs[j][0], start=(j == 0),
                                                            stop=(j == 3)), reads=[obs[j][1]], writes=[g.PSB[6]])
            for j in range(4):
                P.op('pe', lambda e, j=j, obs=obs: e.matmul(g.PS[7], lhsT=g.ones, rhs=obs[j][2], start=(j == 0),
                                                            stop=(j == 3)), reads=[obs[j][3]], writes=[g.PSB[7]])
            mean, meanB = mr.next()
            var, varB = vrr.next()
            P.op('act', lambda e, mean=mean: e.activation(out=mean, in_=g.PS[6], func=AF.Copy, scale=1.0 / 512.0),
                 reads=[g.PSB[6]], writes=[meanB])
            P.op('pool', lambda e, var=var, mean=mean: e.tensor_tensor(out=var, in0=mean, in1=mean, op=ALU.mult),
                 reads=[meanB], writes=[varB])
            P.op('dve', lambda e, var=var: e.scalar_tensor_tensor(out=var, in0=g.PS[7], scalar=1.0 / 512.0, in1=var,
                                                                  op0=ALU.mult, op1=ALU.subtract),
                 reads=[g.PSB[7], varB], writes=[varB])
            P.op('act', lambda e, var=var: e.activation(out=var, in_=var, func=AF.Ln, bias=g.c_eps[:, 2:3]),
                 reads=[varB], writes=[varB])
            P.op('act', lambda e, var=var: e.activation(out=var, in_=var, func=AF.Exp, scale=-0.5), reads=[varB],
                 writes=[varB])
            for j in range(4):
                t1, t1B = t1r.next()
                P.op('dve', lambda e, t1=t1, j=j, mean=mean: e.tensor_tensor(out=t1, in0=g.PS[2 + j], in1=mean,
                                                                            op=ALU.subtract),
                     reads=[g.PSB[2 + j], meanB], writes=[t1B])
                P.op('dve', lambda e, t1=t1, var=var, gn=gn, j=j: e.scalar_tensor_tensor(
                    out=t1, in0=t1, scalar=gn[:, j:j + 1], in1=var, op0=ALU.mult, op1=ALU.mult),
                     reads=[t1B, varB, gnB], writes=[t1B])
                dst = g.bigT[:, (hl * 4 + j) * T + G4 * 512:(hl * 4 + j) * T + (G4 + 1) * 512]
                P.op('pool', lambda e, t1=t1, g4=g4, j=j, dst=dst: e.tensor_tensor(
                    out=dst, in0=t1, in1=g4[:, j * 512:(j + 1) * 512], op=ALU.mult),
                     reads=[t1B, g4B], writes=[g.bigB])
    P.barrier()
    A.release()
    A.mark()
    oproj = OutProj(g, prm['w_out'], RH * 4, [0, 1, 2, 3], prm['hsrc'], prm['oset'], nbuf=4)
    for j in range(4):
        oproj.chunk(j)
    P.barrier(cc=False)
    A.release()
    A.release()


def layer_fox(g, prm):
    P, A = g.P, g.A
    HW = NH * 128
    A.mark()
    c3 = A.alloc(T, BF16)
    cT = A.alloc(NT * 16, F32)
    A.mark()
    cpos = A.alloc(T, F32)
    spl = A.alloc(T, F32)
    hi16 = A.alloc(T, BF16)
    mid16 = A.alloc(T, BF16)
    lo16 = A.alloc(T, BF16)
    onesr = A.alloc(T, F32)
    cB = Buf('cfox')
    c3B = Buf('c3')
    fb = A.alloc(1, F32)
    fbB = Buf('fb')
    P.op('pool', lambda e: e.memset(fb[0:80, :], 0.0), writes=[fbB])
    for o in (0, 32, 64):
        P.dma('sp', fb[o:o + NH, :], prm['fb'].rearrange('o d -> d o'), reads=[fbB], writes=[fbB])
    P.op('pool', lambda e: e.tensor_scalar(out=fb[0:80, :], in0=fb[0:80, :], scalar1=-1.0, scalar2=None, op0=ALU.mult),
         reads=[fbB], writes=[fbB])
    P.op('pool', lambda e: e.memset(onesr[0:80, :], 1.0), writes=[cB])
    P.op('pool', lambda e: e.memset(c3[0:80, :], 0.0), writes=[c3B])
    A.mark()
    gq, gqB = load_col(g, A, prm['qn'], 128, 1.0, 'gq')
    gk, gkB = load_col(g, A, prm['kn'], 128, float(np.sqrt(128.0)), 'gk')
    blocks = []
    for b in range(HW // 512):
        blocks.append((b * 512, 512, [dict(kind='qk', off=j * 128, n=128, row=(b * 4 + j) * 128, gcol=gq, gB=gqB, rope=None)
                                      for j in range(4)]))
    for b in range(HW // 512):
        blocks.append((HW + b * 512, 512, [dict(kind='qk', off=j * 128, n=128, row=HW + (b * 4 + j) * 128, gcol=gk,
                                                gB=gkB, rope=None) for j in range(4)]))
    for b in range(HW // 512):
        blocks.append((2 * HW + b * 512, 512, [dict(kind='tok', off=0, n=512, vcol=b * 512)]))
    for b in range(HW // 512):
        blocks.append((3 * HW + b * 512, 512, [dict(kind='gate', off=j * 128, n=128, row=2 * HW + (b * 4 + j) * 128)
                                               for j in range(4)]))

    def f_loader(wv, wB):
        P.op('pool', lambda e: e.memset(wv[:, :, 0:80], 0.0), writes=[wB])
        for o in (0, 32, 64):
            P.dma('pool', wv[:, :, o:o + NH], prm['w_in'][:, 4 * HW:4 * HW + NH].rearrange('(k p) n -> p k n', p=128),
                  reads=[wB], writes=[wB])

    blocks.append((4 * HW, NH, [dict(kind='fox_f', off=0, n=80, sb_dst=(spl, cB), fbias=fb, fbB=fbB)], f_loader))
    phase_A2(g, prm['w_in'], blocks)
    A.release()
    P.op('dve', lambda e: e.tensor_tensor_scan(out=cpos[0:80, :], data0=onesr[0:80, :], data1=spl[0:80, :], initial=0.0,
                                               op0=ALU.mult, op1=ALU.add), reads=[cB], writes=[cB])
    P.op('dve', lambda e: e.tensor_copy(out=hi16[0:80, :], in_=cpos[0:80, :]), reads=[cB], writes=[cB])
    P.op('dve', lambda e: e.tensor_tensor(out=spl[0:80, :], in0=cpos[0:80, :], in1=hi16[0:80, :], op=ALU.subtract),
         reads=[cB], writes=[cB])
    P.op('dve', lambda e: e.tensor_copy(out=mid16[0:80, :], in_=spl[0:80, :]), reads=[cB], writes=[cB])
    P.op('dve', lambda e: e.tensor_tensor(out=spl[0:80, :], in0=spl[0:80, :], in1=mid16[0:80, :], op=ALU.subtract),
         reads=[cB], writes=[cB])
    P.op('dve', lambda e: e.tensor_copy(out=lo16[0:80, :], in_=spl[0:80, :]), reads=[cB], writes=[cB])
    P.op('pool', lambda e: e.tensor_copy(out=c3[0:16, :], in_=hi16[0:16, :]), reads=[cB, c3B], writes=[c3B])
    P.op('pool', lambda e: e.tensor_copy(out=c3[32:48, :], in_=mid16[32:48, :]), reads=[cB, c3B], writes=[c3B])
    P.op('pool', lambda e: e.tensor_copy(out=c3[64:80, :], in_=lo16[64:80, :]), reads=[cB, c3B], writes=[c3B])
    for kt in range(NT):
        P.op('pe', lambda e, kt=kt: e.transpose(out=g.PS[7][:, kt * 16:(kt + 1) * 16], in_=cpos[0:16, kt * 128:(kt + 1) * 128],
                                                identity=g.identf[0:16, 0:16]), reads=[cB], writes=[g.PSB[7]])
    cTB = Buf('cT')
    P.op('act', lambda e: e.copy(out=cT, in_=g.PS[7][:, 0:NT * 16]), reads=[g.PSB[7]], writes=[cTB])
    P.barrier()
    A.release()
    A.mark()
    kr = Ring(A, 'kh', 2, T, BF16)
    vr = Ring(A, 'vh', 2, NT * 128, BF16)
    q4r = Ring(A, 'q4', 2, 512, BF16)
    g4r = Ring(A, 'g4', 2, 512, BF16)
    ptr = Ring(A, 'pt', 3, 512, BF16)
    rlr = Ring(A, 'rl', 2, 512, F32)
    tmr = Ring(A, 'tm', 2, 512, F32)
    oproj = OutProj(g, prm['w_out'], NH, [6, 7], prm['hsrc'], prm['oset'], scratch=g.bigT[:, NH * T:KC * T].bitcast(F32))
    cnt = 0
    pend_c = None
    for G4 in range(4):
        nkt = 4 * G4 + 4
        for h in range(NH):
            kh, khB = kr.next()
            vh, vhB = vr.next()
            load_kv(g, A, HW + h * 128, 1, h * 128, 1, kh, khB, vh, vhB, nkt=nkt)
            q4, q4B = q4r.next()
            g4, g4B = g4r.next()
            P.dma('sp', q4, g.zT[h * 128:(h + 1) * 128, G4 * 512:(G4 + 1) * 512], reads=[g.zTB], writes=[q4B])
            P.dma('sp', g4, g.zT[2 * HW + h * 128:2 * HW + (h + 1) * 128, G4 * 512:(G4 + 1) * 512], reads=[g.zTB],
                  writes=[g4B])
            kts = []
            for kt in range(nkt):
                m = kt - 4 * G4
                c0 = max(0, m) * 128
                bias = [(g.e3[0:80, h * 128:(h + 1) * 128], c3[0:80, G4 * 512 + c0:(G4 + 1) * 512], c0, 512, [c3B])]
                if m >= 0:
                    bias.append((g.negI, g.ctri, c0, c0 + 128, []))
                kts.append(dict(k=[kh[:, kt * 128:(kt + 1) * 128]], kB=[khB], v=[vh[:, kt * 128:(kt + 1) * 128]], vB=[vhB],
                                c0=c0, bias=bias, pt=pt_exp(cT[:, kt * 16 + h:kt * 16 + h + 1], [cTB])))
            k = cnt % 2
            cnt += 1
            dst = g.bigT[:, h * T + G4 * 512:h * T + (G4 + 1) * 512]
            attend(g, 512, [q4], [q4B], kts, 1, [0, 1], [2 + k], 4 + k, ptr,
                   finish=(lambda k=k, g4=g4, g4B=g4B, dst=dst: softmax_finish_act(g, 512, 2 + k, 4 + k, g4, [g4B], dst,
                                                                             rlr, tmr)))
            if pend_c is not None:
                oproj.chunk(pend_c)
                pend_c = None
        pend_c = G4
        oproj.prefetch(G4)
    oproj.chunk(pend_c)
    P.barrier(cc=False)
    A.release()
    A.release()


W_SHAPES = {
    'a_norm': [1, D], 'a_w_in': [D, 2 * NH * 128 + 2 * NKV * 128 + 1104], 'a_q_norm': [1, 128], 'a_k_norm': [1, 128],
    'a_w_out': [NH * 128, D],
    'b_norm': [1, D], 'b_w_in': [D, 2 * NH * 128 + 2 * NKV * 128], 'b_q_norm': [1, 128], 'b_k_norm': [1, 128],
    'b_w_out': [NH * 128, D],
    'c_norm': [1, D], 'c_w_in': [D, RH * 1536], 'c_gn': [1, RH * 512], 'c_w_out': [RH * 512, D],
    'd_norm': [1, D], 'd_w_in': [D, 4 * NH * 128 + NH], 'd_f_bias': [1, NH], 'd_q_norm': [1, 128], 'd_k_norm': [1, 128],
    'd_w_out': [NH * 128, D],
}
C_SHAPES = {'c_mats': [128, 768], 'c_f32': [128, 264], 'c_e8': [8, 1024], 'c_e3': [80, 2048],
            'rope128': [2, 128, T], 'rope64': [2, 128, T], 'rope256': [2, 128, T], 'ret_w': [RH, 2, 128, 512],
            'ret_sc': [128, RH * 16]}


def build(layers=(0, 1, 2, 3), debug=False):
    nc = bass.Bass("TRN2", target_bir_lowering=False)
    x = nc.dram_tensor("x", [T, D], F32, kind="ExternalInput").ap()
    out = nc.dram_tensor("out", [T, D], F32, kind="ExternalOutput").ap()
    w = {k: nc.dram_tensor(k, s, F32, kind="ExternalInput").ap() for k, s in W_SHAPES.items()}
    cst = {k: nc.dram_tensor(k, s, F32, kind="ExternalInput").ap() for k, s in C_SHAPES.items()}
    dk = dict(kind="ExternalOutput") if debug else {}
    zT = nc.dram_tensor("zT", [4096, T], BF16, **dk).ap()
    vtok = nc.dram_tensor("vtok", [T, 2048], BF16, **dk).ap()
    ctx = ExitStack()
    with ctx:
        g = G()
        g.nc = nc
        g.P = P = Prog(nc, ctx)
        P.flush_hook = lambda: flush_att(g)
        art = ctx.enter_context(nc.sbuf_tensor("arena", [128, ARENA_WORDS], F32))
        g.A = A = Arena(art, ARENA_WORDS)
        pst = [ctx.enter_context(nc.psum_tensor("ps%d" % i, [128, 512], F32)) for i in range(8)]
        g.PS = [t.ap() for t in pst]
        g.PSB = [Buf('ps%d' % i) for i in range(8)]
        g.zT, g.vtok, g.zTB, g.vtB = zT, vtok, Buf('zT'), Buf('vtok')
        g.ypart_t = [nc.dram_tensor("ypart%d" % j, [512, D], F32) for j in range(4)]
        g.ysum_t = [[nc.dram_tensor("ysum%d_%d" % (k, j), [512, D], F32) for j in range(4)] for k in range(2)]
        g.ypart = [t.ap() for t in g.ypart_t]
        g.ysum = [[t.ap() for t in ts] for ts in g.ysum_t]
        g.ypB = [Buf('yp%d' % j) for j in range(4)]
        g.ysB = [[Buf('ys%d_%d' % (k, j)) for j in range(4)] for k in range(2)]
        g.rope128, g.rope64, g.rope256 = cst['rope128'], cst['rope64'], cst['rope256']
        g.bigT = A.alloc(KC * T, BF16)
        g.bigB = Buf('bigT')
        mats = A.alloc(768, BF16)
        cB = Buf('consts')
        P.dma('pool', mats, cst['c_mats'], writes=[cB])
        g.ident, g.negI, g.ones = mats[:, 0:128], mats[:, 128:256], mats[:, 256:384]
        g.rm128, g.rm64, g.ctri = mats[:, 384:512], mats[:, 512:640], mats[:, 640:768]
        cf = A.alloc(264, F32)
        P.dma('sp', cf, cst['c_f32'], writes=[cB])
        g.negtri = cf[:, 0:128]
        g.c_eps = cf[:, 128:131]
        g.identf = cf[:, 136:264]
        g.c_one = A.alloc(1, F32)
        P.op('pool', lambda e: e.memset(g.c_one, 1.0), writes=[cB])
        g.e8 = A.alloc(1024, BF16)
        P.dma('pool', g.e8[0:8, :], cst['c_e8'], writes=[cB])
        g.e3 = A.alloc(2048, BF16)
        P.dma('pool', g.e3[0:80, :], cst['c_e3'], writes=[cB])
        P.barrier()

        hB_x = [Buf('hx%d' % i) for i in range(NT)]
        hB_o = [Buf('ho%d' % i) for i in range(4)]
        norms = {0: 'a_norm', 1: 'b_norm', 2: 'c_norm', 3: 'd_norm'}
        for li, L in enumerate(layers):
            if li == 0:
                hsrc = lambda tt: (x[tt * 128:(tt + 1) * 128, :], hB_x[tt])
            else:
                hsrc = lambda tt, k=(li - 1) % 2: (g.ysum[k][tt // 4][(tt % 4) * 128:(tt % 4 + 1) * 128, :],
                                                   g.ysB[k][tt // 4])
            oset = li % 2
            phase_A1(g, hsrc, w[norms[L]])
            if L == 0:
                layer_dsa(g, dict(w_in=w['a_w_in'], qn=w['a_q_norm'], kn=w['a_k_norm'], w_out=w['a_w_out'], hsrc=hsrc,
                                  oset=oset))
            elif L == 1:
                layer_moba(g, dict(w_in=w['b_w_in'], qn=w['b_q_norm'], kn=w['b_k_norm'], w_out=w['b_w_out'], hsrc=hsrc,
                                   oset=oset))
            elif L == 2:
                layer_ret(g, dict(w_in=w['c_w_in'], gn=w['c_gn'], w_out=w['c_w_out'], ret_w=cst['ret_w'],
                                  ret_sc=cst['ret_sc'], hsrc=hsrc, oset=oset))
            else:
                layer_fox(g, dict(w_in=w['d_w_in'], fb=w['d_f_bias'], qn=w['d_q_norm'], kn=w['d_k_norm'],
                                  w_out=w['d_w_out'], hsrc=hsrc, oset=oset))
        final_copy(g, (len(layers) - 1) % 2, out, hB_o)
        P.barrier()
        P.emit()
        g.stats = dict(peak_words=A.peak, ops={k: len(v) for k, v in P.q.items()})
    return nc, g.stats


def shard_weights(weights, s):
    def cat(a, pieces):
        return np.ascontiguousarray(np.concatenate([a[..., lo:hi] for lo, hi in pieces], axis=-1))
    hq, hk = NH * 128, NKV * 128
    o = {}
    A_ = weights['a_w_in'][0]
    o['a_w_in'] = cat(A_, [(s * hq, (s + 1) * hq), (2048 + s * hk, 2048 + (s + 1) * hk),
                           (2560 + s * hk, 2560 + (s + 1) * hk), (3072 + s * hq, 3072 + (s + 1) * hq), (5120, 6224)])
    o['a_w_out'] = np.ascontiguousarray(weights['a_w_out'][0][s * hq:(s + 1) * hq])
    B_ = weights['b_w_in'][0]
    o['b_w_in'] = cat(B_, [(s * hq, (s + 1) * hq), (2048 + s * hk, 2048 + (s + 1) * hk),
                           (2560 + s * hk, 2560 + (s + 1) * hk), (3072 + s * hq, 3072 + (s + 1) * hq)])
    o['b_w_out'] = np.ascontiguousarray(weights['b_w_out'][0][s * hq:(s + 1) * hq])
    C_ = weights['c_w_in'][0]
    rq, rv = RH * 256, RH * 512
    o['c_w_in'] = cat(C_, [(s * rq, (s + 1) * rq), (2048 + s * rq, 2048 + (s + 1) * rq),
                           (4096 + s * rv, 4096 + (s + 1) * rv), (8192 + s * rv, 8192 + (s + 1) * rv)])
    o['c_gn'] = np.ascontiguousarray(weights['c_gn'][0][s * rv:(s + 1) * rv]).reshape(1, rv)
    o['c_w_out'] = np.ascontiguousarray(weights['c_w_out'][0][s * rv:(s + 1) * rv])
    D_ = weights['d_w_in'][0]
    o['d_w_in'] = cat(D_, [(s * hq, (s + 1) * hq), (2048 + s * hq, 2048 + (s + 1) * hq),
                           (4096 + s * hq, 4096 + (s + 1) * hq), (6144 + s * hq, 6144 + (s + 1) * hq),
                           (8192 + s * NH, 8192 + (s + 1) * NH)])
    o['d_f_bias'] = np.ascontiguousarray(weights['d_f_bias'][0][s * NH:(s + 1) * NH]).reshape(1, NH)
    o['d_w_out'] = np.ascontiguousarray(weights['d_w_out'][0][s * hq:(s + 1) * hq])
    for k in ('a_norm', 'a_q_norm', 'a_k_norm', 'b_norm', 'b_q_norm', 'b_k_norm', 'c_norm', 'd_norm', 'd_q_norm',
              'd_k_norm'):
        o[k] = np.ascontiguousarray(weights[k][0]).reshape(1, -1)
    return o


def shard_consts(cst, s):
    o = {k: cst[k] for k in C_SHAPES if k in cst}
    o['ret_w'] = np.ascontiguousarray(cst['ret_w'][s * RH:(s + 1) * RH])
    lg = cst['ret_lg'][s * RH:(s + 1) * RH]
    sc = np.exp(lg[:, None] * 128.0 * np.arange(16)[None, :]).astype(np.float32).reshape(1, RH * 16)
    o['ret_sc'] = np.ascontiguousarray(np.broadcast_to(sc, (128, RH * 16)))
    return o


_CACHE = {}
LAST = None


def run_layers(x4, weights, layers=(0, 1, 2, 3), n_cores=8, debug=False):
    global LAST
    key = (tuple(layers), debug)
    if key not in _CACHE:
        _CACHE[key] = build(layers, debug)
    nc, stats = _CACHE[key]
    cst = host_consts()
    B = x4.shape[0]
    wsh = [shard_weights(weights, s) for s in range(TP)]
    csh = [shard_consts(cst, s) for s in range(TP)]
    in_maps = []
    for c in range(n_cores):
        b, s = (c // TP) % B, c % TP
        m = {'x': np.ascontiguousarray(x4[b], dtype=np.float32)}
        for k in W_SHAPES:
            m[k] = np.ascontiguousarray(wsh[s][k], dtype=np.float32).reshape(W_SHAPES[k])
        for k in C_SHAPES:
            m[k] = np.ascontiguousarray(csh[s][k], dtype=np.float32).reshape(C_SHAPES[k])
        in_maps.append(m)
    res = run_bass_kernel_spmd(nc, in_maps, core_ids=list(range(n_cores)))
    LAST = res
    return np.stack([res.results[TP * b]['out'] for b in range(B)])


def kernel(**inputs):
    x = np.asarray(inputs['x'], dtype=np.float32)
    weights = {k: np.asarray(inputs[k], dtype=np.float32) for k in
               ('a_norm', 'a_w_in', 'a_q_norm', 'a_k_norm', 'a_w_out', 'b_norm', 'b_w_in', 'b_q_norm', 'b_k_norm',
                'b_w_out', 'c_norm', 'c_w_in', 'c_gn', 'c_w_out', 'd_norm', 'd_w_in', 'd_f_bias', 'd_q_norm',
                'd_k_norm', 'd_w_out')}
    return run_layers(x, weights).astype(np.float32)
```
